# Optimizing a Trainium2 kernel written in Bass

```python
import jax
import jax.numpy as jnp
from jax import lax
import numpy as np

D_MODEL = 1024
BATCH = 4
SEQ = 4096
DEPTH = 1
DEC_BATCH = 8
DEC_SEQ = 8192
PAST_LEN = 128

HEAD_DIM = 64
ATTN_HEADS = 8
ATTN_WIDTH = ATTN_HEADS * HEAD_DIM
DILATION_PATTERNS = ((128, 1), (512, 4), (2048, 16))
ATTN_BLOCK = 64
ROPE_THETA = 10000.0
GLA_HEADS = 4
GLA_DK = 64
GLA_DV = 128
GLA_KW = GLA_HEADS * GLA_DK
GLA_VW = GLA_HEADS * GLA_DV
GLA_GATE_RANK = 16
GLA_TAU = 16.0
GLA_CHUNK = 64
MIX_WIDTH = ATTN_WIDTH + GLA_VW
D_FF = -(-8 * D_MODEL // (3 * 256)) * 256
EPS = 1e-6
MASK_VALUE = -1e30
SPLIT_SIZES = (ATTN_WIDTH, ATTN_WIDTH, ATTN_WIDTH, GLA_KW, GLA_KW, GLA_VW, GLA_VW, GLA_GATE_RANK, GLA_GATE_RANK)
IN_COLS = sum(SPLIT_SIZES)

kernel_name = "hybrid_dilated_attn_gla_encoder"


def rmsnorm(x, g):
    xf = x.astype(jnp.float32)
    y = xf * lax.rsqrt(jnp.mean(xf * xf, axis=-1, keepdims=True) + EPS)
    return (y * g.astype(jnp.float32)).astype(x.dtype)


def rope(x):
    s, dh = x.shape[1], x.shape[-1]
    inv_freq = ROPE_THETA ** (-jnp.arange(0, dh, 2, dtype=jnp.float32) / dh)
    ang = jnp.arange(s, dtype=jnp.float32)[:, None] * inv_freq[None, :]
    cos = jnp.cos(ang)[None, :, None, :]
    sin = jnp.sin(ang)[None, :, None, :]
    xf = x.astype(jnp.float32)
    x1, x2 = xf[..., : dh // 2], xf[..., dh // 2:]
    return jnp.concatenate([x1 * cos - x2 * sin, x2 * cos + x1 * sin], axis=-1).astype(x.dtype)


def dilated_pattern(q, k, v, window, dilation):
    b, s, h, dh = q.shape
    d = dilation
    half = window // (2 * d)
    L = s // d
    nb = -(-L // ATTN_BLOCK)
    lp = nb * ATTN_BLOCK

    def to_sub(t):
        t = t.reshape(b, L, d, h, dh).transpose(0, 2, 3, 1, 4)
        return jnp.pad(t, ((0, 0), (0, 0), (0, 0), (0, lp - L), (0, 0)))

    def neighbours(t):
        t = jnp.pad(to_sub(t), ((0, 0), (0, 0), (0, 0), (ATTN_BLOCK, ATTN_BLOCK), (0, 0)))
        t = t.reshape(b, d, h, nb + 2, ATTN_BLOCK, dh)
        return jnp.concatenate([t[:, :, :, :-2], t[:, :, :, 1:-1], t[:, :, :, 2:]], axis=4)

    qb = to_sub(q).reshape(b, d, h, nb, ATTN_BLOCK, dh)
    kb = neighbours(k)
    vb = neighbours(v)
    qpos = jnp.arange(nb)[:, None] * ATTN_BLOCK + jnp.arange(ATTN_BLOCK)[None, :]
    kpos = (jnp.arange(nb)[:, None] - 1) * ATTN_BLOCK + jnp.arange(3 * ATTN_BLOCK)[None, :]
    dist = kpos[:, None, :] - qpos[:, :, None]
    valid = (jnp.abs(dist) <= half) & (kpos[:, None, :] >= 0) & (kpos[:, None, :] < L)
    scores = jnp.einsum('bdhnqe,bdhnke->bdhnqk', qb, kb,
                        preferred_element_type=jnp.float32) * (dh ** -0.5)
    scores = jnp.where(valid, scores, MASK_VALUE)
    lse = jax.nn.logsumexp(scores, axis=-1)
    p = jnp.exp(scores - lse[..., None])
    o = jnp.einsum('bdhnqk,bdhnke->bdhnqe', p, vb.astype(jnp.float32))
    o = o.reshape(b, d, h, lp, dh)[:, :, :, :L].transpose(0, 3, 1, 2, 4).reshape(b, s, h, dh)
    lse = lse.reshape(b, d, h, lp)[..., :L].transpose(0, 3, 1, 2).reshape(b, s, h)
    return o, lse


def dilated_mixture(q, k, v):
    outs, lses = [], []
    for window, dilation in DILATION_PATTERNS:
        o, l = dilated_pattern(q, k, v, window, dilation)
        outs.append(o)
        lses.append(l)
    w = jax.nn.softmax(jnp.stack(lses, axis=0), axis=0)
    return jnp.sum(w[..., None] * jnp.stack(outs, axis=0), axis=0)


def gla_chunked(q, k, v, log_a, include_diag):
    b, h, s, dk = q.shape
    dv = v.shape[-1]
    n = s // GLA_CHUNK
    q = q.reshape(b, h, n, GLA_CHUNK, dk)
    k = k.reshape(b, h, n, GLA_CHUNK, dk)
    v = v.reshape(b, h, n, GLA_CHUNK, dv)
    cum = jnp.cumsum(log_a.reshape(b, h, n, GLA_CHUNK, dk), axis=3)
    cum_last = cum[:, :, :, -1:, :]
    q_e = q * jnp.exp(cum)
    k_e = k * jnp.exp(-cum)
    k_end = k * jnp.exp(cum_last - cum)
    tri = jnp.tril(jnp.ones((GLA_CHUNK, GLA_CHUNK), dtype=bool), 0 if include_diag else -1)
    attn = jnp.where(tri, jnp.einsum('bhnce,bhnse->bhncs', q_e, k_e), 0.0)
    o_intra = jnp.einsum('bhncs,bhnsv->bhncv', attn, v)
    chunk_kv = jnp.einsum('bhnse,bhnsv->bhnev', k_end, v)
    chunk_decay = jnp.exp(cum_last[:, :, :, 0])

    def step(state, inp):
        dec, kv = inp
        return state * dec[..., None] + kv, state

    init = jnp.zeros((b, h, dk, dv), jnp.float32)
    _, states = lax.scan(step, init, (jnp.moveaxis(chunk_decay, 2, 0), jnp.moveaxis(chunk_kv, 2, 0)))
    states = jnp.moveaxis(states, 0, 2)
    o_inter = jnp.einsum('bhnce,bhnev->bhncv', q_e, states)
    return (o_intra + o_inter).reshape(b, h, s, dv)


def hybrid_mixer(hn, w_in, gate_up_fwd, gate_bias_fwd, gate_up_bwd, gate_bias_bwd, gla_norm_g, w_out):
    b, s, _ = hn.shape
    f32 = jnp.float32
    proj = hn @ w_in
    idx = np.cumsum(SPLIT_SIZES)[:-1].tolist()
    qa, ka, va, qg, kg, vg, og, rf, rb = jnp.split(proj, idx, axis=-1)

    qa = rope(qa.reshape(b, s, ATTN_HEADS, HEAD_DIM))
    ka = rope(ka.reshape(b, s, ATTN_HEADS, HEAD_DIM))
    va = va.reshape(b, s, ATTN_HEADS, HEAD_DIM)
    attn = dilated_mixture(qa, ka, va).reshape(b, s, ATTN_WIDTH)

    def to_bhsd(t):
        return t.reshape(b, s, GLA_HEADS, -1).transpose(0, 2, 1, 3).astype(f32)

    def log_decay(r, up, bias):
        z = jnp.einsum('bsr,rk->bsk', r.astype(f32), up.astype(f32)) + bias.astype(f32)
        return to_bhsd(jax.nn.log_sigmoid(z) / GLA_TAU)

    qg = to_bhsd(qg) * (GLA_DK ** -0.5)
    kg = to_bhsd(kg)
    vg = to_bhsd(vg)
    la_f = log_decay(rf, gate_up_fwd, gate_bias_fwd)
    la_b = log_decay(rb, gate_up_bwd, gate_bias_bwd)
    fwd = gla_chunked(qg, kg, vg, la_f, True)
    flip = lambda t: jnp.flip(t, axis=2)
    bwd = flip(gla_chunked(flip(qg), flip(kg), flip(vg), flip(la_b), False))
    o = (fwd + bwd).transpose(0, 2, 1, 3)
    o = o * lax.rsqrt(jnp.mean(o * o, axis=-1, keepdims=True) + EPS)
    o = o * gla_norm_g.astype(f32).reshape(GLA_HEADS, GLA_DV)
    gla = o.reshape(b, s, GLA_VW) * jax.nn.silu(og.astype(f32))

    mixed = jnp.concatenate([attn, gla], axis=-1).astype(hn.dtype)
    return mixed @ w_out


def swiglu(hn, w_gate, w_up, w_down):
    return (jax.nn.silu(hn @ w_gate) * (hn @ w_up)) @ w_down


def trunk(x, norm1_g, w_in, gate_up_fwd, gate_bias_fwd, gate_up_bwd, gate_bias_bwd,
          gla_norm_g, w_out, norm2_g, w_gate, w_up, w_down, final_norm_g):
    for l in range(DEPTH):
        x = x + hybrid_mixer(rmsnorm(x, norm1_g[l]), w_in[l], gate_up_fwd[l], gate_bias_fwd[l],
                             gate_up_bwd[l], gate_bias_bwd[l], gla_norm_g[l], w_out[l])
        x = x + swiglu(rmsnorm(x, norm2_g[l]), w_gate[l], w_up[l], w_down[l])
    return rmsnorm(x, final_norm_g)


def setup_inputs(seed: int = 0) -> dict:
    key = jax.random.key(seed)
    ks = jax.random.split(key, 16)
    f32 = jnp.float32
    nrm = lambda k, shape, scale: jax.random.normal(k, shape, f32) * scale
    return {
        "x_prompt": nrm(ks[0], (BATCH, SEQ, D_MODEL), 1.0),
        "x_sample": nrm(ks[1], (DEC_BATCH, DEC_SEQ, D_MODEL), 1.0),
        "norm1_g": 1.0 + nrm(ks[2], (DEPTH, D_MODEL), 0.02),
        "w_in": nrm(ks[3], (DEPTH, D_MODEL, IN_COLS), D_MODEL ** -0.5),
        "gate_up_fwd": nrm(ks[4], (DEPTH, GLA_GATE_RANK, GLA_KW), GLA_GATE_RANK ** -0.5),
        "gate_bias_fwd": nrm(ks[5], (DEPTH, GLA_KW), 0.1),
        "gate_up_bwd": nrm(ks[6], (DEPTH, GLA_GATE_RANK, GLA_KW), GLA_GATE_RANK ** -0.5),
        "gate_bias_bwd": nrm(ks[7], (DEPTH, GLA_KW), 0.1),
        "gla_norm_g": 1.0 + nrm(ks[8], (DEPTH, GLA_VW), 0.02),
        "w_out": nrm(ks[9], (DEPTH, MIX_WIDTH, D_MODEL), MIX_WIDTH ** -0.5),
        "norm2_g": 1.0 + nrm(ks[10], (DEPTH, D_MODEL), 0.02),
        "w_gate": nrm(ks[11], (DEPTH, D_MODEL, D_FF), D_MODEL ** -0.5),
        "w_up": nrm(ks[12], (DEPTH, D_MODEL, D_FF), D_MODEL ** -0.5),
        "w_down": nrm(ks[13], (DEPTH, D_FF, D_MODEL), D_FF ** -0.5),
        "final_norm_g": 1.0 + nrm(ks[14], (D_MODEL,), 0.02),
    }


def reference(x_prompt, x_sample, norm1_g, w_in, gate_up_fwd, gate_bias_fwd, gate_up_bwd, gate_bias_bwd,
              gla_norm_g, w_out, norm2_g, w_gate, w_up, w_down, final_norm_g):
    y_prompt = trunk(x_prompt, norm1_g, w_in, gate_up_fwd, gate_bias_fwd, gate_up_bwd, gate_bias_bwd,
                     gla_norm_g, w_out, norm2_g, w_gate, w_up, w_down, final_norm_g)
    y_sample = trunk(x_sample, norm1_g, w_in, gate_up_fwd, gate_bias_fwd, gate_up_bwd, gate_bias_bwd,
                     gla_norm_g, w_out, norm2_g, w_gate, w_up, w_down, final_norm_g)
    return (y_prompt, y_sample)
```

```python
import contextlib
import numpy as np
import concourse.bass as bass
import concourse.mybir as mybir
from concourse.bass_utils import run_bass_kernel_spmd

F32 = mybir.dt.float32
BF16 = mybir.dt.bfloat16
AF = mybir.ActivationFunctionType
ALU = mybir.AluOpType

ENGS = ("pe", "act", "dve", "pool", "sp")
D = 1024
DFF = 2816
NFF = DFF // 128
NWIN = 3360
PADR = 1024
EPS = 1e-6


class Buf:
    __slots__ = ("name", "w", "r", "dsem")

    def __init__(self, name):
        self.name = name
        self.w = {}
        self.r = {}
        self.dsem = None


import os
KSTOP = int(os.environ.get("KSTOP", "0"))


class _Stop(Exception):
    pass


def sl(start, n, step):
    return slice(start, start + (n - 1) * step + 1, step)


def stop_at(n):
    if KSTOP == n:
        raise _Stop()


class _Rec:
    def __getattr__(self, name):
        def f(*a, **k):
            self.call = (name, a, k)
            return self
        return f


class Sched:
    def __init__(self, nc, stack, n_dsem=56):
        self.nc = nc
        self.q = {e: [] for e in ENGS}
        self.cnt = {e: 0 for e in ENGS}
        self.sems = {e: stack.enter_context(nc.semaphore("prog_" + e)) for e in ENGS}
        self.seen = {e: {} for e in ENGS}
        self.dtotal = {}
        self.free_dsem = []
        for i in range(n_dsem):
            k = "d%d" % i
            self.sems[k] = stack.enter_context(nc.semaphore(k))
            self.dtotal[k] = 0
            self.free_dsem.append(k)
        self.phase_bufs = []
        self.fresh_dsem = []
        for i in range(14):
            k = "dsw%d" % i
            self.sems[k] = stack.enter_context(nc.semaphore(k))
            self.dtotal[k] = 0
            self.fresh_dsem.append(k)

    def _need(self, eng, key, val, waits):
        if val <= 0 or self.seen[eng].get(key, 0) >= val:
            return
        self.seen[eng][key] = val
        waits[key] = max(waits.get(key, 0), val)

    def _deps(self, eng, reads, writes, waits, same_engine_raw=True):
        for b in reads:
            for k, v in b.w.items():
                if k == eng and (eng == "pe" or not same_engine_raw):
                    continue
                self._need(eng, k, v, waits)
        for b in writes:
            for k, v in b.w.items():
                if k == eng and (eng == "pe" or not same_engine_raw):
                    continue
                self._need(eng, k, v, waits)
            for k, v in b.r.items():
                if k == eng and (eng == "pe" or not same_engine_raw):
                    continue
                self._need(eng, k, v, waits)

    def op(self, eng, fn, reads=(), writes=()):
        rec = _Rec()
        fn(rec)
        name, a, k = rec.call
        fn = (lambda e, name=name, a=a, k=k: getattr(e, name)(*a, **k))
        waits = {}
        self._deps(eng, reads, writes, waits)
        self.cnt[eng] += 1
        seq = self.cnt[eng]
        for b in reads:
            b.r[eng] = seq
        for b in writes:
            b.w[eng] = seq
            b.r = {}
        self.q[eng].append((list(waits.items()), fn, (eng, 1)))

    def dma(self, eng, pairs, sbuf, reads=(), writes=()):
        if sbuf.dsem is None:
            if eng == "pool":
                sbuf.dsem = self.fresh_dsem.pop()
            else:
                sbuf.dsem = self.free_dsem.pop()
                self.phase_bufs.append(sbuf)
        key = sbuf.dsem
        waits = {}
        self._need(eng, key, self.dtotal[key], waits)
        self._deps(eng, reads, writes, waits, same_engine_raw=False)
        self.dtotal[key] += 16 * len(pairs)
        tot = self.dtotal[key]
        for b in reads:
            b.r[key] = tot
        for b in writes:
            b.w[key] = tot
            b.r = {}
        first = True
        for (o, i) in pairs:
            w = list(waits.items()) if first else []
            first = False
            self.q[eng].append((w, (lambda e, o=o, i=i: e.dma_start(out=o, in_=i)), (key, 16)))

    def dma_custom(self, eng, fn, sbuf, reads=(), writes=()):
        if sbuf.dsem is None:
            sbuf.dsem = self.fresh_dsem.pop() if eng == "pool" else self.free_dsem.pop()
            if eng != "pool":
                self.phase_bufs.append(sbuf)
        key = sbuf.dsem
        waits = {}
        self._need(eng, key, self.dtotal[key], waits)
        self._deps(eng, reads, writes, waits, same_engine_raw=False)
        self.dtotal[key] += 16
        tot = self.dtotal[key]
        for b in reads:
            b.r[key] = tot
        for b in writes:
            b.w[key] = tot
            b.r = {}
        self.q[eng].append((list(waits.items()), fn, (key, 16)))

    def barrier(self):
        tgt = dict(self.cnt)
        tgt.update(self.dtotal)
        for e in ENGS:
            waits = {}
            for k, v in tgt.items():
                if k != e:
                    self._need(e, k, v, waits)
            self.cnt[e] += 1
            self.q[e].append((list(waits.items()), (lambda eo: eo.nop()), (e, 1)))
        for b in self.phase_bufs:
            if b.dsem is not None:
                self.free_dsem.append(b.dsem)
                b.dsem = None
        self.phase_bufs = []

    def emit(self):
        nc = self.nc
        engobj = {"pe": "tensor", "act": "scalar", "dve": "vector", "pool": "gpsimd", "sp": "sync"}
        final = [(k, v) for k, v in self.dtotal.items() if v > 0]
        with nc.Block() as block:
            for e in ENGS:
                items = self.q[e]

                def body(eobj, items=items, e=e):
                    for waits, fn, (ikey, inc) in items:
                        for k, v in waits:
                            eobj.wait_ge(self.sems[k], v)
                        fn(eobj).then_inc(self.sems[ikey], inc)
                    if e == "sp":
                        for k, v in final:
                            eobj.wait_ge(self.sems[k], v)

                getattr(block, engobj[e])(body)


def build_program(seq_lens, phases="ABCD", debug=False, d_lens=None):
    d_lens = list(d_lens) if d_lens is not None else list(seq_lens)
    nc = bass.Bass("TRN2", target_bir_lowering=False)
    nseq = len(seq_lens)
    dt_in = lambda name, shape, dt=F32: nc.dram_tensor(name, shape, dt, kind="ExternalInput").ap()
    dt_out = lambda name, shape, dt=F32: nc.dram_tensor(name, shape, dt, kind="ExternalOutput").ap()
    dt_tmp = lambda name, shape, dt: nc.dram_tensor(name, shape, dt, kind=("ExternalOutput" if debug else "Internal")).ap()

    xs = [dt_in("x%d" % i, [s, D]) for i, s in enumerate(seq_lens)]
    ys = [dt_out("y%d" % i, [s, D]) for i, s in enumerate(d_lens)]
    ridx_d = [nc.dram_tensor("ridx%d" % i, [128, s // 128], mybir.dt.int32, kind="ExternalInput").ap() for i, s in enumerate(d_lens)]
    cosd = [dt_in("cos%d" % i, [128, s]) for i, s in enumerate(seq_lens)]
    sind = [dt_in("sin%d" % i, [128, s]) for i, s in enumerate(seq_lens)]
    win_d = dt_in("w_in_p", [D, NWIN])
    wout_d = dt_in("w_out", [D, D])
    wg_d = dt_in("w_gate", [D, DFF])
    wu_d = dt_in("w_up", [D, DFF])
    wd_d = dt_in("w_down", [DFF, D])
    upf_d = dt_in("gate_up_fwd", [16, 256])
    upb_d = dt_in("gate_up_bwd", [16, 256])
    bf_d = dt_in("gate_bias_fwd", [1, 256])
    bb_d = dt_in("gate_bias_bwd", [1, 256])
    g1_d = dt_in("norm1_g", [1, D])
    g2_d = dt_in("norm2_g", [1, D])
    gf_d = dt_in("final_norm_g", [1, D])
    gg_d = dt_in("gla_norm_g", [128, 4])

    scr = []
    for i, s in enumerate(seq_lens):
        scr.append(dict(
            qT=dt_tmp("qT%d" % i, [8, 64, s], BF16),
            kT=dt_tmp("kT%d" % i, [8, 64, s], BF16),
            v=dt_tmp("v%d" % i, [s + 2 * PADR, 8 * 128], BF16),
            oP=dt_tmp("oP%d" % i, [4, 128, s], F32),
            qeb=dt_tmp("qeb%d" % i, [2, 128, s], BF16),
            kvb=dt_tmp("kvb%d" % i, [s // 512, 128, 4 * 2 * 128], F32),
            sog=dt_tmp("sog%d" % i, [4, 128, s], BF16),
            mixA=dt_tmp("mixA%d" % i, [4, 128, s], BF16),
            x1=dt_tmp("x1_%d" % i, [s, D], F32),
        ))

    with contextlib.ExitStack() as st:
        S = Sched(nc, st)
        G = st
        G0 = G

        def sbt(stack, name, shape, dt):
            return stack.enter_context(nc.sbuf_tensor(name, shape, dt))

        def pst(stack, name, shape, dt=F32):
            return stack.enter_context(nc.psum_tensor(name, shape, dt))


        ident = sbt(G, "ident", [128, 128], BF16); ident_b = Buf("ident")
        cf = sbt(G, "cf", [128, 5, 128], F32); cf_b = Buf("cf")
        ones_bf = sbt(G, "ones_bf", [128, 128], BF16); ones_b = Buf("ones")
        gmask = sbt(G, "gmask", [128, 2, 2, 128], F32); gmask_b = Buf("gmask")
        amask = sbt(G, "amask", [128, 256], BF16); amask_b = Buf("amask")
        epsb = sbt(G, "epsb", [128, 1], F32); eps_b = Buf("eps")
        oneb = sbt(G, "oneb", [128, 1], F32); one_b = Buf("one")
        upbm = sbt(G, "upbm", [33, 512], BF16); upbm_b = Buf("upbm")
        ggs = sbt(G, "ggs", [128, 4], F32); ggs_b = Buf("ggs")
        decb = [sbt(G, "decb%d" % i, [128, s // 128, 2], F32) for i, s in enumerate(seq_lens)]
        decb_b = [Buf("decb%d" % i) for i in range(nseq)]
        amask_f = sbt(G0, "amask_f", [128, 256], F32)
        upst = sbt(G0, "upst", [33, 512], F32); upst_b = Buf("upst")
        zero_bf = sbt(G0, "zero_bf", [128, 256], BF16); zero_b = Buf("zero")

        P = S.op
        P("pool", lambda e: e.memset(cf[:], 1.0), writes=[cf_b])
        P("pool", lambda e: e.affine_select(out=cf[:, 0, :], in_=cf[:, 0, :], pattern=[[-1, 128]], compare_op=ALU.is_equal, fill=0.0, base=0, channel_multiplier=1), reads=[cf_b], writes=[cf_b])
        P("pool", lambda e: e.affine_select(out=cf[:, 1, :], in_=cf[:, 1, :], pattern=[[1, 128]], compare_op=ALU.is_ge, fill=0.0, base=0, channel_multiplier=-1), reads=[cf_b], writes=[cf_b])
        P("pool", lambda e: e.affine_select(out=cf[:, 2, :], in_=cf[:, 2, :], pattern=[[1, 128]], compare_op=ALU.is_gt, fill=0.0, base=0, channel_multiplier=-1), reads=[cf_b], writes=[cf_b])
        P("pool", lambda e: e.affine_select(out=cf[:, 3, :], in_=cf[:, 3, :], pattern=[[-1, 128]], compare_op=ALU.is_ge, fill=0.0, base=0, channel_multiplier=1), reads=[cf_b], writes=[cf_b])
        P("pool", lambda e: e.affine_select(out=cf[:, 4, :], in_=cf[:, 4, :], pattern=[[-1, 128]], compare_op=ALU.is_gt, fill=0.0, base=0, channel_multiplier=1), reads=[cf_b], writes=[cf_b])
        P("dve", lambda e: e.tensor_copy(out=ident[:], in_=cf[:, 0, :]), reads=[cf_b], writes=[ident_b])
        for hh_ in range(2):
            P("dve", lambda e: e.tensor_copy(out=gmask[:, hh_, 0, :], in_=cf[:, 1, :]), reads=[cf_b], writes=[gmask_b])
            P("dve", lambda e: e.tensor_copy(out=gmask[:, hh_, 1, :], in_=cf[:, 4, :]), reads=[cf_b], writes=[gmask_b])
        P("pool", lambda e: e.memset(ones_bf[:], 1.0), writes=[ones_b])
        P("pool", lambda e: e.memset(zero_bf[:], 0.0), writes=[zero_b])
        P("pool", lambda e: e.memset(epsb[:], EPS), writes=[eps_b])
        P("pool", lambda e: e.memset(oneb[:], 1.0), writes=[one_b])
        P("pool", lambda e: e.memset(amask_f[:], 1.0), writes=[amask_b])
        P("pool", lambda e: e.affine_select(out=amask_f[:], in_=amask_f[:], pattern=[[1, 256]], compare_op=ALU.is_ge, fill=0.0, base=0, channel_multiplier=-1), reads=[amask_b], writes=[amask_b])
        P("pool", lambda e: e.affine_select(out=amask_f[:], in_=amask_f[:], pattern=[[-1, 256]], compare_op=ALU.is_ge, fill=0.0, base=128, channel_multiplier=1), reads=[amask_b], writes=[amask_b])
        P("dve", lambda e: e.tensor_copy(out=amask[:], in_=amask_f[:]), reads=[amask_b], writes=[amask_b])
        P("pool", lambda e: e.memset(upst[:], 0.0), writes=[upst_b])
        S.dma("sp", [(upst[0:16, 0:256], upf_d), (upst[16:32, 256:512], upb_d),
                     (upst[32:33, 0:256], bf_d), (upst[32:33, 256:512], bb_d)], upst_b, writes=[upst_b])
        P("dve", lambda e: e.tensor_copy(out=upbm[:], in_=upst[:]), reads=[upst_b], writes=[upbm_b])
        S.dma("sp", [(ggs[:], gg_d)], ggs_b, writes=[ggs_b])

        for i, s in enumerate(seq_lens):
            vd = scr[i]["v"]
            S.dma("sp", [(vd[k * 128:(k + 1) * 128, c_ * 256:(c_ + 1) * 256], zero_bf[:]) for k in range(PADR // 128) for c_ in range(4)] +
                        [(vd[PADR + s + k * 128:PADR + s + (k + 1) * 128, c_ * 256:(c_ + 1) * 256], zero_bf[:]) for k in range(PADR // 128) for c_ in range(4)],
                  zero_b, reads=[zero_b])


        if "A" in phases:
          with contextlib.ExitStack() as pa:
            win = sbt(pa, "win", [128, 8, NWIN], BF16); win_b = Buf("win")
            g1t = sbt(pa, "g1t", [128, D], F32); g1_b = Buf("g1t")
            S.dma("pool", [(win[:, k, :], win_d[k * 128:(k + 1) * 128, :]) for k in range(8)], win_b, writes=[win_b])
            S.dma("sp", [(g1t[:], g1_d.partition_broadcast(128))], g1_b, writes=[g1_b])

            xt = [sbt(pa, "xt%d" % i, [128, D], F32) for i in range(3)]; xt_b = [Buf("xt%d" % i) for i in range(3)]
            ss = sbt(pa, "ss", [128, 2], F32); ss_b = [Buf("ss0"), Buf("ss1")]
            rstd = sbt(pa, "rstd", [128, 2], F32); rstd_b = [Buf("rstd0"), Buf("rstd1")]
            hn = [sbt(pa, "hn%d" % i, [128, D], BF16) for i in range(2)]; hn_b = [Buf("hn0"), Buf("hn1")]
            hnT = [sbt(pa, "hnT%d" % i, [128, 8, 512], BF16) for i in range(2)]; hnT_b = [Buf("hnT0"), Buf("hnT1")]
            cst = [sbt(pa, "cst%d" % i, [128, 2, 512], F32) for i in range(2)]; cst_b = [Buf("cst0"), Buf("cst1")]
            rps = [sbt(pa, "rp%d" % i, [128, 4, 512], F32) for i in range(2)]; rps_b = [Buf("rp0"), Buf("rp1")]
            ro = [sbt(pa, "ro%d" % i, [128, 2, 512], BF16) for i in range(2)]; ro_b = [Buf("ro0"), Buf("ro1")]
            qgs = [sbt(pa, "qgs%d" % i, [128, 2, 512], F32) for i in range(2)]; qgs_b = [Buf("qgs0"), Buf("qgs1")]
            kgs = [sbt(pa, "kgs%d" % i, [128, 2, 512], F32) for i in range(2)]; kgs_b = [Buf("kgs0"), Buf("kgs1")]
            sogs = sbt(pa, "sogs", [128, 4, 512], BF16); sogs_b = Buf("sogs")
            rT = [sbt(pa, "rT%d" % i, [33, 512], BF16) for i in range(2)]; rT_b = [Buf("rT0"), Buf("rT1")]
            vga = [sbt(pa, "vga%d" % i, [128, 4, 512], BF16) for i in range(2)]; vga_b = [[Buf("vga%d_%d" % (i, j)) for j in range(4)] for i in range(2)]
            kgt = [sbt(pa, "kgt%d" % i, [128, 4, 256], F32) for i in range(2)]; kgt_b = [[Buf("kgt%d_%d" % (i, j)) for j in range(4)] for i in range(2)]
            vst = [sbt(pa, "vst%d" % i, [128, 8, 128], BF16) for i in range(2)]; vst_b = [Buf("vst0"), Buf("vst1")]
            Lt = sbt(pa, "Lt", [128, 512], F32); Lt_b = Buf("Lt")
            ek = sbt(pa, "ek", [128, 2, 256], F32); ek_b = Buf("ek")
            kend = [sbt(pa, "kend%d" % i, [128, 2, 256], BF16) for i in range(2)]; kend_b = [Buf("kend0"), Buf("kend1")]
            epm = sbt(pa, "epm", [128, 2, 4, 128], F32); epm_b = Buf("epm")
            qe = [sbt(pa, "qe%d" % i, [128, 2, 2, 512], BF16) for i in range(2)]; qe_b = [Buf("qe0"), Buf("qe1")]
            ke = [sbt(pa, "ke%d" % i, [128, 2, 2, 2, 128], BF16) for i in range(2)]; ke_b = [Buf("ke0"), Buf("ke1")]
            ats = sbt(pa, "ats", [128, 2, 2, 128], BF16); ats_b = Buf("ats")
            oPs = sbt(pa, "oPs", [128, 4, 512], F32); oPs_b = Buf("oPs")
            kvbs = sbt(pa, "kvbs", [128, 4, 2, 128], F32); kvbs_b = Buf("kvbs")
            stf = sbt(pa, "stf", [128, 2, 128], F32); stf_b = Buf("stf")
            stfb = sbt(pa, "stfb", [128, 2, 2, 128], BF16); stfb_b = Buf("stfb")
            decf = [sbt(pa, "decf%d" % i, [128, 2], F32) for i in range(2)]; decf_b = [Buf("decf0"), Buf("decf1")]

            for i in range(2):
                P("pool", lambda e: e.memset(rT[i][:], 1.0), writes=[rT_b[i]])
                P("pool", lambda e: e.memset(vst[i][:], 1.0), writes=[vst_b[i]])
            for i in range(2):
                P("pool", lambda e: e.memset(ke[i][:], 0.0), writes=[ke_b[i]])

            PB = [None] + [pst(pa, "pa%d" % i, [128, 512], F32) for i in range(1, 8)]
            PBb = [Buf("pb%d" % i) for i in range(8)]
            pT16t = pst(pa, "paT", [128, 1024], BF16); pT_b = PBb[0]
            pT16 = pT16t[:]
            acc = [PB[1], PB[2], PB[4]]; acc_b = [PBb[1], PBb[2], PBb[4]]
            acc_i = [0]

            def next_acc():
                i = acc_i[0] % 3
                acc_i[0] += 1
                return acc[i], acc_b[i]
            g1p, g1p_b = PB[3], PBb[3]
            g2p, g2p_b = PB[3], PBb[3]
            g3p, g3p_b = PB[5], PBb[5]
            g4p, g4p_b = PB[6], PBb[6]
            g5p, g5p_b = PB[7], PBb[7]

            xcount = [0]
            rpi = [0]
            vcnt = [0]

            def X(si, t):
                sc = scr[si]
                slen = seq_lens[si]
                ntile = slen // 512
                tb = t % 2
                t0 = t * 512
                S.dma("sp", [(cst[tb][:, 0, :], cosd[si][:, t0:t0 + 512]), (cst[tb][:, 1, :], sind[si][:, t0:t0 + 512])],
                      cst_b[tb], writes=[cst_b[tb]])
                nxt = t + 1 < ntile
                sched = {}
                if nxt:
                    for j in range(3):
                        load_xt(si, t + 1, j)
                    sched = {2: ("n", 0), 4: ("t", 0), 5: ("l", 3), 7: ("n", 1), 9: ("t", 1), 11: ("n", 2), 13: ("t", 2), 15: ("n", 3), 17: ("t", 3)}
                stepc = [0]

                def tick():
                    a_ = sched.get(stepc[0])
                    stepc[0] += 1
                    if a_ is None:
                        return
                    if a_[0] == "l":
                        load_xt(si, t + 1, a_[1])
                    elif a_[0] == "n":
                        norm_ops(si, t + 1, a_[1])
                    else:
                        transp_ops(t + 1, a_[1])

                def fm_group(gcol, ncols=128):
                    a, ab = next_acc()
                    for k in range(8):
                        P("pe", lambda e: e.matmul(a[0:ncols, :], lhsT=win[:, k, gcol:gcol + ncols], rhs=hnT[tb][:, k, :], start=(k == 0), stop=(k == 7)),
                          reads=[win_b, hnT_b[tb]], writes=[ab])
                    return a, ab

                for qk in range(2):
                    dst = sc["qT"] if qk == 0 else sc["kT"]
                    for half in range(2):
                        gbase = (qk * 4 + half * 2) * 128
                        rp, rp_b = rps[rpi[0] % 2], rps_b[rpi[0] % 2]
                        rb = rpi[0] % 2
                        rpi[0] += 1
                        a1, a1b = fm_group(gbase)
                        P("act", lambda e: e.activation(out=rp[:, 0, :], in_=a1[:], func=AF.Copy), reads=[a1b], writes=[rp_b])
                        tick()
                        yield
                        a2, a2b = fm_group(gbase + 128)
                        P("act", lambda e: e.activation(out=rp[:, 1, :], in_=a2[:], func=AF.Copy), reads=[a2b], writes=[rp_b])
                        cs, sn = cst[tb][:, 0, :], cst[tb][:, 1, :]
                        P("dve", lambda e: e.tensor_tensor(out=rp[:, 2, :], in0=rp[:, 0, :], in1=cs, op=ALU.mult), reads=[rp_b, cst_b[tb]], writes=[rp_b])
                        P("pool", lambda e: e.tensor_tensor(out=rp[:, 3, :], in0=rp[:, 1, :], in1=sn, op=ALU.mult), reads=[rp_b, cst_b[tb]], writes=[rp_b])
                        P("dve", lambda e: e.tensor_tensor(out=ro[rb][:, 0, :], in0=rp[:, 2, :], in1=rp[:, 3, :], op=ALU.subtract), reads=[rp_b], writes=[ro_b[rb]])
                        P("dve", lambda e: e.tensor_tensor(out=rp[:, 2, :], in0=rp[:, 1, :], in1=cs, op=ALU.mult), reads=[rp_b, cst_b[tb]], writes=[rp_b])
                        P("pool", lambda e: e.tensor_tensor(out=rp[:, 3, :], in0=rp[:, 0, :], in1=sn, op=ALU.mult), reads=[rp_b, cst_b[tb]], writes=[rp_b])
                        P("dve", lambda e: e.tensor_tensor(out=ro[rb][:, 1, :], in0=rp[:, 2, :], in1=rp[:, 3, :], op=ALU.add), reads=[rp_b], writes=[ro_b[rb]])
                        pairs = []
                        for hq in range(4):
                            h = half * 4 + hq
                            pairs.append((dst[h, 0:32, t0:t0 + 512], ro[rb][hq * 32:(hq + 1) * 32, 0, :]))
                            pairs.append((dst[h, 32:64, t0:t0 + 512], ro[rb][hq * 32:(hq + 1) * 32, 1, :]))
                        S.dma("sp", pairs, ro_b[rb], reads=[ro_b[rb]])
                        tick()
                        yield
                for g in range(2):
                    a, ab = fm_group((8 + g) * 128)
                    P("act", lambda e: e.activation(out=qgs[tb][:, g, :], in_=a[:], func=AF.Copy, scale=0.125), reads=[ab], writes=[qgs_b[tb]])
                    tick()
                    yield
                for g in range(2):
                    a, ab = fm_group((10 + g) * 128)
                    P("act", lambda e: e.activation(out=kgs[tb][:, g, :], in_=a[:], func=AF.Copy), reads=[ab], writes=[kgs_b[tb]])
                    tick()
                    yield
                for h in range(4):
                    a, ab = fm_group((12 + h) * 128)
                    P("act", lambda e: e.activation(out=sogs[:, h, :], in_=a[:], func=AF.Silu), reads=[ab], writes=[sogs_b])
                    tick()
                S.dma("sp", [(sc["sog"][h, :, t0:t0 + 512], sogs[:, h, :]) for h in range(4)], sogs_b, reads=[sogs_b])
                a, ab = fm_group(16 * 128, 32)
                P("act", lambda e: e.activation(out=rT[tb][0:32, :], in_=a[0:32, :], func=AF.Copy), reads=[ab], writes=[rT_b[tb]])
                yield
                for j in range(4):
                    tok = slice(j * 128, (j + 1) * 128)
                    a, ab = next_acc()
                    for k in range(8):
                        P("pe", lambda e: e.matmul(a[:], lhsT=hnT[tb][:, k, tok], rhs=win[:, k, 2080:2592], start=(k == 0), stop=(k == 7)),
                          reads=[win_b, hnT_b[tb]], writes=[ab])
                    vb = vcnt[0] % 2
                    vcnt[0] += 1
                    P("act", lambda e: e.activation(out=vst[vb][:, :, 0:64], in_=a[:].rearrange("p (h d) -> p h d", h=8), func=AF.Copy),
                      reads=[ab], writes=[vst_b[vb]])
                    r0 = PADR + t0 + j * 128
                    S.dma("sp", [(sc["v"][r0:r0 + 128, :], vst[vb][:].rearrange("p h d -> p (h d)"))], vst_b[vb], reads=[vst_b[vb]])
                    tick()
                    yield
                    a, ab = next_acc()
                    for k in range(8):
                        P("pe", lambda e: e.matmul(a[:], lhsT=hnT[tb][:, k, tok], rhs=win[:, k, 2592:3104], start=(k == 0), stop=(k == 7)),
                          reads=[win_b, hnT_b[tb]], writes=[ab])
                    P("act", lambda e: e.activation(out=vga[tb][:, j, :], in_=a[:], func=AF.Copy), reads=[ab], writes=[vga_b[tb][j]])
                    tick()
                    yield
                    a, ab = next_acc()
                    for k in range(8):
                        P("pe", lambda e: e.matmul(a[:, 0:256], lhsT=hnT[tb][:, k, tok], rhs=win[:, k, 3104:3360], start=(k == 0), stop=(k == 7)),
                          reads=[win_b, hnT_b[tb]], writes=[ab])
                    P("dve", lambda e: e.tensor_copy(out=kgt[tb][:, j, :], in_=a[:, 0:256]), reads=[ab], writes=[kgt_b[tb][j]])
                    tick()
                    yield

            xslot_a = {}

            def load_xt(si, t, j):
                b = xcount[0] % 3
                xcount[0] += 1
                xslot_a[(si, t, j)] = b
                r0 = t * 512 + j * 128
                S.dma("sp", [(xt[b][:], xs[si][r0:r0 + 128, :])], xt_b[b], writes=[xt_b[b]])

            def norm_ops(si, t, j):
                xb = xslot_a[(si, t, j)]
                sj = j % 2
                P("act", lambda e: e.activation(out=hn[sj][:], in_=xt[xb][:], func=AF.Square, accum_out=ss[:, sj:sj + 1]),
                  reads=[xt_b[xb]], writes=[hn_b[sj], ss_b[sj]])
                P("act", lambda e: e.activation(out=rstd[:, sj:sj + 1], in_=ss[:, sj:sj + 1], func=AF.Ln, scale=1.0 / D, bias=epsb[:]),
                  reads=[ss_b[sj], eps_b], writes=[rstd_b[sj]])
                P("act", lambda e: e.activation(out=rstd[:, sj:sj + 1], in_=rstd[:, sj:sj + 1], func=AF.Exp, scale=-0.5),
                  reads=[rstd_b[sj]], writes=[rstd_b[sj]])
                P("dve", lambda e: e.scalar_tensor_tensor(out=hn[sj][:], in0=xt[xb][:], scalar=rstd[:, sj:sj + 1], in1=g1t[:], op0=ALU.mult, op1=ALU.mult),
                  reads=[xt_b[xb], rstd_b[sj], g1_b], writes=[hn_b[sj]])

            def transp_ops(t, j):
                tb = t % 2
                sj = j % 2
                for k in range(8):
                    P("pe", lambda e: e.transpose(out=pT16[:, k * 128:(k + 1) * 128], in_=hn[sj][:, k * 128:(k + 1) * 128], identity=ident[:]),
                      reads=[hn_b[sj], ident_b], writes=[pT_b])
                P("dve", lambda e: e.tensor_copy(out=hnT[tb][:, :, j * 128:(j + 1) * 128], in_=pT16.rearrange("p (k t) -> p k t", k=8)), reads=[pT_b], writes=[hnT_b[tb]])

            def Y(si, t):
                sc = scr[si]
                tb = t % 2
                t0 = t * 512

                def Ya(j):
                    ch = t * 4 + j
                    tok = slice(j * 128, (j + 1) * 128)
                    kb = j % 2
                    P("pe", lambda e: e.matmul(g1p[:], lhsT=rT[tb][0:33, tok], rhs=upbm[0:33, :], start=True, stop=True), reads=[rT_b[tb], upbm_b], writes=[g1p_b])
                    P("act", lambda e: e.activation(out=Lt[:], in_=g1p[:], func=AF.Exp, scale=-1.0), reads=[g1p_b], writes=[Lt_b])
                    P("act", lambda e: e.activation(out=Lt[:], in_=Lt[:], func=AF.Ln, bias=oneb[:]), reads=[Lt_b, one_b], writes=[Lt_b])
                    yield
                    P("pe", lambda e: e.matmul(g1p[:, 0:256], lhsT=cf[:, 4, :], rhs=Lt[:, 0:256], start=True, stop=True), reads=[cf_b, Lt_b], writes=[g1p_b])
                    P("pe", lambda e: e.matmul(g1p[:, 256:512], lhsT=cf[:, 2, :], rhs=Lt[:, 256:512], start=True, stop=True), reads=[cf_b, Lt_b], writes=[g1p_b])
                    P("act", lambda e: e.activation(out=ek[:].rearrange("p a b -> p (a b)"), in_=g1p[:], func=AF.Exp, scale=-1.0 / 16), reads=[g1p_b], writes=[ek_b])
                    P("dve", lambda e: e.tensor_tensor(out=kend[kb][:], in0=ek[:], in1=kgt[tb][:, j, :].unsqueeze(1).to_broadcast([128, 2, 256]), op=ALU.mult),
                      reads=[ek_b, kgt_b[tb][j]], writes=[kend_b[kb]])
                    yield
                    for d_ in range(2):
                        for g in range(2):
                            col = d_ * 256 + g * 128
                            P("pe", lambda e: e.matmul(g2p[:, (d_ * 2 + g) * 128:(d_ * 2 + g + 1) * 128], lhsT=Lt[:, col:col + 128], rhs=cf[:, 1 if d_ == 0 else 3, :], start=True, stop=True),
                              reads=[cf_b, Lt_b], writes=[g2p_b])
                    P("act", lambda e: e.activation(out=epm[:, 0, :, :].rearrange("p a t -> p (a t)"), in_=g2p[:], func=AF.Exp, scale=-1.0 / 16), reads=[g2p_b], writes=[epm_b])
                    P("act", lambda e: e.activation(out=epm[:, 1, :, :].rearrange("p a t -> p (a t)"), in_=g2p[:], func=AF.Exp, scale=1.0 / 16), reads=[g2p_b], writes=[epm_b])
                    yield
                    P("dve", lambda e: e.tensor_tensor(out=qe[tb][:, :, :, tok], in0=epm[:, 0, :, :].rearrange("p (d g) t -> p d g t", d=2),
                                                       in1=qgs[tb][:, :, tok].unsqueeze(1).to_broadcast([128, 2, 2, 128]), op=ALU.mult),
                      reads=[epm_b, qgs_b[tb]], writes=[qe_b[tb]])
                    for hh_ in range(2):
                        pr_ = slice(64 * hh_, 64 * hh_ + 64)
                        P("dve", lambda e: e.tensor_tensor(out=ke[kb][pr_, hh_, :, :, :], in0=epm[pr_, 1, :, :].rearrange("p (d g) t -> p d g t", d=2),
                                                           in1=kgs[tb][pr_, :, tok].unsqueeze(1).to_broadcast([64, 2, 2, 128]), op=ALU.mult),
                          reads=[epm_b, kgs_b[tb]], writes=[ke_b[kb]])
                    P("act", lambda e: e.activation(out=decf[kb][:], in_=epm[:, 0, 0:2, 127], func=AF.Copy), reads=[epm_b], writes=[decf_b[kb]])
                    P("act", lambda e: e.activation(out=decb[si][:, ch, :], in_=epm[:, 0, 2:4, 0], func=AF.Copy), reads=[epm_b], writes=[decb_b[si]])
                    yield

                def Yb(j):
                    tok = slice(j * 128, (j + 1) * 128)
                    kb = j % 2
                    vgs = vga[tb][:, j, :]
                    vgs_b = vga_b[tb][j]
                    for g in range(2):
                        for hh in range(2):
                            for d_ in range(2):
                                P("pe", lambda e: e.matmul(g3p[:, (hh * 2 + d_) * 128:(hh * 2 + d_ + 1) * 128], lhsT=ke[kb][:, hh, d_, g, :], rhs=qe[tb][:, d_, g, tok], start=True, stop=True),
                                  reads=[ke_b[kb], qe_b[tb]], writes=[g3p_b])
                        P("dve", lambda e: e.tensor_tensor(out=ats[:], in0=g3p[:].rearrange("p (h d c) -> p h d c", h=2, d=2), in1=gmask[:], op=ALU.mult),
                          reads=[g3p_b, gmask_b], writes=[ats_b])
                        if g == 0:
                            for g_ in range(2):
                                P("pe", lambda e: e.matmul(g5p[:, g_ * 256:(g_ + 1) * 256], lhsT=kend[kb][:, 0, g_ * 128:(g_ + 1) * 128], rhs=vgs[:, g_ * 256:(g_ + 1) * 256], start=True, stop=True),
                                  reads=[kend_b[kb], vgs_b], writes=[g5p_b])
                        yield
                        for hh in range(2):
                            h = g * 2 + hh
                            oo = g4p[:, h * 128:(h + 1) * 128]
                            P("pe", lambda e: e.matmul(oo, lhsT=vgs[:, h * 128:(h + 1) * 128], rhs=ats[:, hh, 0, :], start=True, stop=False), reads=[vgs_b, ats_b], writes=[g4p_b])
                            P("pe", lambda e: e.matmul(oo, lhsT=vgs[:, h * 128:(h + 1) * 128], rhs=ats[:, hh, 1, :], start=False, stop=False), reads=[vgs_b, ats_b], writes=[g4p_b])
                            P("pe", lambda e: e.matmul(oo, lhsT=stfb[:, hh, g, :], rhs=qe[tb][:, 0, g, tok], start=False, stop=True), reads=[stfb_b, qe_b[tb]], writes=[g4p_b])
                        if g == 0:
                            yield
                    P("act", lambda e: e.activation(out=oPs[:, :, tok], in_=g4p[:].rearrange("p (h c) -> p h c", h=4), func=AF.Copy), reads=[g4p_b], writes=[oPs_b])
                    for g in range(2):
                        for hh in range(2):
                            pr = slice(64 * hh, 64 * hh + 64)
                            src = g5p[pr, g * 256 + hh * 128:g * 256 + hh * 128 + 128]
                            P("dve", lambda e: e.scalar_tensor_tensor(out=stf[pr, g, :], in0=stf[pr, g, :], scalar=decf[kb][pr, g:g + 1], in1=src, op0=ALU.mult, op1=ALU.add),
                              reads=[stf_b, decf_b[kb], g5p_b], writes=[stf_b])
                    for hh_ in range(2):
                        pr_ = slice(64 * hh_, 64 * hh_ + 64)
                        P("act", lambda e: e.activation(out=stfb[pr_, hh_, :, :], in_=stf[pr_, :, :], func=AF.Copy), reads=[stf_b], writes=[stfb_b])
                    yield
                    for g in range(2):
                        P("pe", lambda e: e.matmul(g5p[:, g * 256:(g + 1) * 256], lhsT=kend[kb][:, 1, g * 128:(g + 1) * 128], rhs=vgs[:, g * 256:(g + 1) * 256], start=True, stop=True),
                          reads=[kend_b[kb], vgs_b], writes=[g5p_b])
                    for g in range(2):
                        for hh in range(2):
                            pr = slice(64 * hh, 64 * hh + 64)
                            src = g5p[pr, g * 256 + hh * 128:g * 256 + hh * 128 + 128]
                            P("act", lambda e: e.activation(out=kvbs[pr, j, g, :], in_=src, func=AF.Copy), reads=[g5p_b], writes=[kvbs_b])
                    yield

                def zipgen(ga, gb):
                    gs = [g for g in (ga, gb) if g is not None]
                    while gs:
                        for g in list(gs):
                            try:
                                next(g)
                            except StopIteration:
                                gs.remove(g)
                                continue
                            yield

                yield from zipgen(Ya(0), None)
                for j in range(4):
                    yield from zipgen(Ya(j + 1) if j + 1 < 4 else None, Yb(j))
                S.dma("sp", [(sc["oP"][h, :, t0:t0 + 512], oPs[:, h, :]) for h in range(4)], oPs_b, reads=[oPs_b])
                S.dma("sp", [(sc["qeb"][g, :, t0:t0 + 512], qe[tb][:, 1, g, :]) for g in range(2)], qe_b[tb], reads=[qe_b[tb]])
                S.dma("sp", [(sc["kvb"][t, :, :], kvbs[:].rearrange("p a g d -> p (a g d)"))], kvbs_b, reads=[kvbs_b])
                yield

            def run_interleaved(gens):
                gens = [g for g in gens if g is not None]
                while gens:
                    for g in list(gens):
                        try:
                            next(g)
                        except StopIteration:
                            gens.remove(g)

            for si, slen in enumerate(seq_lens):
                ntile = slen // 512
                P("pool", lambda e: e.memset(stf[:], 0.0), writes=[stf_b])
                P("pool", lambda e: e.memset(stfb[:], 0.0), writes=[stfb_b])
                for j in range(3):
                    load_xt(si, 0, j)
                for j in range(4):
                    norm_ops(si, 0, j)
                    transp_ops(0, j)
                    if j == 0:
                        load_xt(si, 0, 3)
                run_interleaved([X(si, 0)])
                for t in range(ntile):
                    run_interleaved([X(si, t + 1) if t + 1 < ntile else None, Y(si, t)])
            S.barrier()

        if "B" in phases:
          with contextlib.ExitStack() as pb:
            SMAX = max(seq_lens)
            KW = SMAX + 2 * PADR
            qT2 = [sbt(pb, "qT2_%d" % i, [128, SMAX], BF16) for i in range(2)]; qT2_b = [Buf("qT2_0"), Buf("qT2_1")]
            qd4 = sbt(pb, "qd4", [128, SMAX], BF16); qd4_b = Buf("qd4")
            qd16 = sbt(pb, "qd16", [128, SMAX], BF16); qd16_b = Buf("qd16")
            kAB = [sbt(pb, "k%s" % n, [128, KW], BF16) for n in "AB"]
            kAB_b = [Buf("k%s" % n) for n in "AB"]
            accs = [sbt(pb, "acc%d" % i, [128, SMAX], F32) for i in range(2)]
            accs_cb = [[Buf("acc%d_%d" % (i, c_)) for c_ in range(SMAX // 512)] for i in range(2)]
            NV = 6
            vts = [sbt(pb, "vt%d" % i, [128, 2, 128], BF16) for i in range(NV)]; vts_b = [Buf("vt%d" % i) for i in range(NV)]
            NS = 4
            mixo = [sbt(pb, "mixo%d" % i, [128, SMAX // 4], BF16) for i in range(4)]; mixo_b = [Buf("mixo%d" % i) for i in range(4)]
            deferred = []
            ptm = [sbt(pb, "ptm%d" % i, [128, 2, 256], BF16) for i in range(NS)]; ptm_b = [Buf("ptm%d" % i) for i in range(NS)]
            am2 = sbt(pb, "am2", [128, 256], BF16); am2_b = Buf("am2")
            rdp = pst(pb, "rdp", [128, 512], F32); rdp_b = Buf("rdp")
            lnt, lnt_b = rdp, rdp_b
            rds = [sbt(pb, "rds%d" % i, [64, 512], F32) for i in range(2)]; rds_b = [Buf("rds0"), Buf("rds1")]
            nrc = [0]
            sTp = [pst(pb, "sT%d" % i, [128, 2, 256], F32) for i in range(NS)]; sTp_b = [Buf("sT%d" % i) for i in range(NS)]
            NSL = 3
            oPp = [pst(pb, "oPp%d" % i, [128, 2, 256], F32) for i in range(NSL)]
            oPp_b = [Buf("oPp%d" % i) for i in range(NSL)]

            P("dve", lambda e: e.tensor_scalar(out=am2[:], in0=amask[:], scalar1=-1.0, scalar2=240000.0, op0=ALU.add, op1=ALU.mult), reads=[amask_b], writes=[am2_b])
            for n in range(2):
                P("pool", lambda e: e.memset(kAB[n][:], 0.0), writes=[kAB_b[n]])

            pair_list = [(si, p) for si in range(nseq) for p in range(4)]

            def load_q(idx):
                si, p = pair_list[idx]
                slen = seq_lens[si]
                b = idx % 2
                S.dma("sp", [(qT2[b][:, 0:slen], scr[si]["qT"][2 * p:2 * p + 2].rearrange("h d s -> (h d) s"))], qT2_b[b], writes=[qT2_b[b]])

            def load_k(idx):
                si, p = pair_list[idx]
                slen = seq_lens[si]
                sc = scr[si]
                if idx > 0 and pair_list[idx - 1][0] != si:
                    for n in range(2):
                        P("dve", lambda e: e.memset(kAB[n][:, PADR + slen:PADR + slen + PADR], 0.0), writes=[kAB_b[n]])
                S.dma("sp", [(kAB[0][0:64, PADR:PADR + slen], sc["kT"][2 * p])], kAB_b[0], writes=[kAB_b[0]])
                S.dma("sp", [(kAB[1][64:128, PADR:PADR + slen], sc["kT"][2 * p + 1])], kAB_b[1], writes=[kAB_b[1]])

            load_q(0)
            vcount = [0]
            for idx, (si, p) in enumerate(pair_list):
                slen = seq_lens[si]
                sc = scr[si]
                b = idx % 2
                S4 = slen // 4
                if idx == 0:
                    load_k(0)
                def destride4(idx_):
                    si_, _p = pair_list[idx_]
                    s4_ = seq_lens[si_] // 4
                    for r in range(4):
                        P("pool", lambda e: e.tensor_copy(out=qd4[:, r * s4_:(r + 1) * s4_], in_=qT2[idx_ % 2][:, sl(r, s4_, 4)]), reads=[qT2_b[idx_ % 2]], writes=[qd4_b])

                if idx == 0:
                    destride4(0)
                S16 = slen // 16
                for r in range(16):
                    P("pool", lambda e: e.tensor_copy(out=qd16[:, r * S16:(r + 1) * S16], in_=qT2[b][:, sl(r, S16, 16)]), reads=[qT2_b[b]], writes=[qd16_b])
                if idx + 1 < len(pair_list):
                    load_q(idx + 1)
                items = []
                for d in (1, 4, 16):
                    L = slen // d
                    NQ = L // 128
                    for r in range(d):
                        for j in range(NQ + 1):
                            items.append((d, r, j, NQ))
                vslot = {}

                def load_v(ii):
                    d, r, j, NQ = items[ii]
                    vb = vcount[0] % NV
                    vcount[0] += 1
                    vslot[ii] = vb
                    row0 = PADR + r + d * (128 * j - 64)
                    src = sc["v"][sl(row0, 128, d), 2 * p * 128:(2 * p + 2) * 128]
                    S.dma("sp", [(vts[vb][:].rearrange("p h c -> p (h c)"), src)], vts_b[vb], writes=[vts_b[vb]])

                def qrange(d, r, j, NQ):
                    lo = max(j - 1, 0)
                    hi = min(j, NQ - 1)
                    c_lo = (lo - (j - 1)) * 128
                    c_hi = (hi - (j - 1) + 1) * 128
                    return lo, hi, c_lo, c_hi

                def scores(ii):
                    d, r, j, NQ = items[ii]
                    lo, hi, c_lo, c_hi = qrange(d, r, j, NQ)
                    sb_ = ii % NS
                    k0 = PADR + r + d * (128 * j - 64)
                    nq = (hi - lo + 1) * 128
                    if d == 1:
                        qsrc, qb_ = qT2[b][:, 128 * lo:128 * lo + nq], qT2_b[b]
                    elif d == 4:
                        qsrc, qb_ = qd4[:, r * S4 + 128 * lo:r * S4 + 128 * lo + nq], qd4_b
                    else:
                        qsrc, qb_ = qd16[:, r * S16 + 128 * lo:r * S16 + 128 * lo + nq], qd16_b
                    for hh in range(2):
                        P("pe", lambda e: e.matmul(sTp[sb_][:, hh, c_lo:c_hi], lhsT=kAB[hh][:, sl(k0, 128, d)], rhs=qsrc, start=True, stop=False),
                          reads=[kAB_b[hh], qb_], writes=[sTp_b[sb_]])
                        P("pe", lambda e: e.matmul(sTp[sb_][:, hh, c_lo:c_hi], lhsT=ident[:], rhs=am2[:, c_lo:c_hi], start=False, stop=True),
                          reads=[ident_b, am2_b], writes=[sTp_b[sb_]])

                def acc_bufs(hh, d, r, qt):
                    if d == 1:
                        return [accs_cb[hh][(r4 * S4 + 32 * qt) // 512] for r4 in range(4)]
                    if d == 4:
                        return [accs_cb[hh][(r * S4 + 128 * qt) // 512]]
                    r4 = r % 4
                    return [accs_cb[hh][(r4 * S4 + 4 * 128 * qt) // 512]]

                def acc_view(hh, d, r, qt):
                    a3 = accs[hh][:, 0:slen].rearrange("p (r m) -> p r m", r=4)
                    if d == 1:
                        return a3[:, :, 32 * qt:32 * qt + 32]
                    if d == 4:
                        return accs[hh][:, r * S4 + 128 * qt:r * S4 + 128 * qt + 128]
                    r4, bq = r % 4, r // 4
                    return accs[hh][:, sl(r4 * S4 + 4 * 128 * qt + bq, 128, 4)]

                def st_exp(ii):
                    d, r, j, NQ = items[ii]
                    lo, hi, c_lo, c_hi = qrange(d, r, j, NQ)
                    sb_ = ii % NS
                    P("act", lambda e: e.activation(out=ptm[sb_][:, :, c_lo:c_hi], in_=sTp[sb_][:, :, c_lo:c_hi], func=AF.Exp, scale=0.125),
                      reads=[sTp_b[sb_]], writes=[ptm_b[sb_]])

                def st_mask(ii):
                    return

                    d, r, j, NQ = items[ii]
                    lo, hi, c_lo, c_hi = qrange(d, r, j, NQ)
                    sb_ = ii % NS
                    P("dve", lambda e: e.tensor_tensor(out=ptm[sb_][:, :, c_lo:c_hi], in0=pts[sb_][:, :, c_lo:c_hi], in1=am2[:, :, c_lo:c_hi], op=ALU.mult),
                      reads=[pts_b[sb_], am2_b], writes=[ptm_b[sb_]])

                def st_pv(ii):
                    d, r, j, NQ = items[ii]
                    lo, hi, c_lo, c_hi = qrange(d, r, j, NQ)
                    sb_ = ii % NS
                    vb = vslot[ii]
                    for qt in range(lo, hi + 1):
                        cq = (qt - (j - 1)) * 128
                        for hh in range(2):
                            P("pe", lambda e: e.matmul(oPp[qt % NSL][:, hh, 0:128], lhsT=vts[vb][:, hh, :], rhs=ptm[sb_][:, hh, cq:cq + 128],
                                                       start=(qt == j and hh == 0), stop=(qt == j - 1 and hh == 1)),
                              reads=[vts_b[vb], ptm_b[sb_]], writes=[oPp_b[qt % NSL]])
                    if j >= 1:
                        qt = j - 1
                        for hh in range(2):
                            src = oPp[qt % NSL][:, hh, 0:128]
                            if d == 1:
                                P("dve", lambda e: e.tensor_copy(out=acc_view(hh, d, r, qt), in_=src.rearrange("p (m r) -> p r m", r=4)),
                                  reads=[oPp_b[qt % NSL]], writes=acc_bufs(hh, d, r, qt))
                            else:
                                P("dve", lambda e: e.tensor_tensor(out=acc_view(hh, d, r, qt), in0=acc_view(hh, d, r, qt), in1=src, op=ALU.add),
                                  reads=[oPp_b[qt % NSL]] + acc_bufs(hh, d, r, qt), writes=acc_bufs(hh, d, r, qt))

                NI = len(items)
                for ii in range(min(4, NI)):
                    load_v(ii)
                for fn_ in deferred:
                    fn_()
                del deferred[:]
                for step in range(-3, NI):
                    if step >= 0 and step + 4 < NI:
                        load_v(step + 4)
                    if 0 <= step + 3 < NI:
                        scores(step + 3)
                    if 0 <= step + 2 < NI:
                        st_exp(step + 2)
                    if 0 <= step + 1 < NI:
                        st_mask(step + 1)
                    if 0 <= step < NI:
                        st_pv(step)
                    if idx + 1 < len(pair_list) and 0 <= step + 3 < NI and items[step + 3][0] == 16 and (step + 3 == 0 or items[step + 2][0] != 16):
                        destride4(idx + 1)
                if idx + 1 < len(pair_list):
                    load_k(idx + 1)
                mixn, mixn_b = qd16, qd16_b
                nchk = slen // 512
                per_r = nchk // 4
                order = [r4 * per_r + i for i in range(per_r) for r4 in range(4)]
                for blk in order:
                    cs_ = slice(blk * 512, (blk + 1) * 512)
                    for hh in range(2):
                        nb_ = nrc[0] % 2
                        nrc[0] += 1
                        P("act", lambda e: e.activation(out=lnt[64:128, :], in_=accs[hh][64:128, cs_], func=AF.Ln), reads=[accs_cb[hh][blk]], writes=[lnt_b])
                        P("act", lambda e: e.activation(out=rds[nb_][0:64, :], in_=lnt[64:128, :], func=AF.Exp, scale=-1.0), reads=[lnt_b], writes=[rds_b[nb_]])
                        P("dve", lambda e: e.tensor_tensor(out=mixn[64 * hh:64 * hh + 64, cs_], in0=accs[hh][0:64, cs_], in1=rds[nb_][0:64, :], op=ALU.mult),
                          reads=[accs_cb[hh][blk], rds_b[nb_]], writes=[mixn_b])
                NCH_ = 4
                mch = S4 // NCH_
                for c_ in range(NCH_):
                    mo = c_
                    for r in range(4):
                        P("pool", lambda e: e.tensor_copy(out=mixo[mo][:, sl(r, mch, 4)], in_=mixn[:, r * S4 + c_ * mch:r * S4 + (c_ + 1) * mch]),
                          reads=[mixn_b], writes=[mixo_b[mo]])
                    deferred.append((lambda sc=sc, p=p, c_=c_, mch=mch, mo=mo: S.dma("sp", [(sc["mixA"][p, :, 4 * c_ * mch:4 * (c_ + 1) * mch], mixo[mo][:, 0:4 * mch])], mixo_b[mo], reads=[mixo_b[mo]])))
            for fn_ in deferred:
                fn_()
            del deferred[:]
            S.barrier()

        wg = sbt(G, "wg", [128, 8, DFF], BF16); wg_b = Buf("wg")
        if "D" in phases:
            S.dma("pool", [(wg[:, k, :], wg_d[k * 128:(k + 1) * 128, :]) for k in range(8)], wg_b, writes=[wg_b])

        if "C" in phases:
          with contextlib.ExitStack() as pc:
            wout = sbt(pc, "wout", [128, 8, D], BF16); wout_b = Buf("wout")
            S.dma("pool", [(wout[:, k, :], wout_d[k * 128:(k + 1) * 128, :]) for k in range(8)], wout_b, writes=[wout_b])
            oPt = [sbt(pc, "oPt%d" % i, [128, 4, 512], F32) for i in range(2)]; oPt_b = [Buf("oPt0"), Buf("oPt1")]
            qebt = [sbt(pc, "qebt%d" % i, [128, 2, 512], BF16) for i in range(2)]; qebt_b = [Buf("qebt0"), Buf("qebt1")]
            kvbt = [sbt(pc, "kvbt%d" % i, [128, 4, 2, 128], F32) for i in range(2)]; kvbt_b = [Buf("kvbt0"), Buf("kvbt1")]
            sogt = [sbt(pc, "sogt%d" % i, [128, 4, 512], BF16) for i in range(2)]; sogt_b = [Buf("sogt0"), Buf("sogt1")]
            mixAt = [sbt(pc, "mixAt%d" % i, [128, 4, 512], BF16) for i in range(2)]; mixAt_b = [Buf("mixAt0"), Buf("mixAt1")]
            xc = [sbt(pc, "xc%d" % i, [128, D], F32) for i in range(8)]; xc_b = [Buf("xc%d" % i) for i in range(8)]
            mixG = [sbt(pc, "mixG%d" % i, [128, 4, 512], BF16) for i in range(2)]; mixG_b = [Buf("mixG0"), Buf("mixG1")]
            stb = sbt(pc, "stb", [128, 2, 128], F32); stb_b = Buf("stb")
            stbb = [sbt(pc, "stbb%d" % i, [128, 2, 2, 128], BF16) for i in range(4)]; stbb_b = [Buf("stbb%d" % i) for i in range(4)]
            osum = [sbt(pc, "osum%d" % i, [128, 4, 128], F32) for i in range(2)]; osum_b = [Buf("osum0"), Buf("osum1")]
            osq = sbt(pc, "osq", [128, 4, 128], BF16); osq_b = Buf("osq")
            grs = sbt(pc, "grs", [128, 4, 128], F32); grs_b = Buf("grs")
            OTp = [pst(pc, "OTp%d" % i, [128, 4, 128], F32) for i in range(2)]; OTp_b = [Buf("OTp0"), Buf("OTp1")]
            SSp = pst(pc, "SSp", [128, 4, 128], F32); SSp_b = Buf("SSp")
            Yp = [pst(pc, "Yp%d" % i, [128, 512], F32) for i in range(4)]; Yp_b = [Buf("Yp%d" % i) for i in range(4)]
            ycount = [0]

            def load_tile(si, t):
                sc = scr[si]
                tb = t % 2
                t0 = t * 512
                S.dma("sp", [(oPt[tb][:, h, :], sc["oP"][h, :, t0:t0 + 512]) for h in range(4)], oPt_b[tb], writes=[oPt_b[tb]])
                S.dma("sp", [(qebt[tb][:, g, :], sc["qeb"][g, :, t0:t0 + 512]) for g in range(2)], qebt_b[tb], writes=[qebt_b[tb]])
                S.dma("sp", [(kvbt[tb][:].rearrange("p a g d -> p (a g d)"), sc["kvb"][t, :, :])], kvbt_b[tb], writes=[kvbt_b[tb]])
                S.dma("sp", [(sogt[tb][:, h, :], sc["sog"][h, :, t0:t0 + 512]) for h in range(4)], sogt_b[tb], writes=[sogt_b[tb]])
                S.dma("sp", [(mixAt[tb][:, h, :], sc["mixA"][h, :, t0:t0 + 512]) for h in range(4)], mixAt_b[tb], writes=[mixAt_b[tb]])
                for j in range(4):
                    xb = (t % 2) * 4 + j
                    S.dma("sp", [(xc[xb][:], xs[si][t0 + j * 128:t0 + (j + 1) * 128, :])], xc_b[xb], writes=[xc_b[xb]])

            def Yc(si, t):
                tb = t % 2

                def states(j):
                    ch = t * 4 + j
                    for hh in range(2):
                        pr = slice(64 * hh, 64 * hh + 64)
                        P("dve", lambda e: e.tensor_copy(out=stbb[j][pr, hh, :, :], in_=stb[pr, :, :]), reads=[stb_b], writes=[stbb_b[j]])
                    for g in range(2):
                        P("dve", lambda e: e.scalar_tensor_tensor(out=stb[:, g, :], in0=stb[:, g, :], scalar=decb[si][:, ch, g:g + 1], in1=kvbt[tb][:, j, g, :], op0=ALU.mult, op1=ALU.add),
                          reads=[stb_b, decb_b[si], kvbt_b[tb]], writes=[stb_b])

                def s1(j):
                    tok = slice(j * 128, (j + 1) * 128)
                    ob = j % 2
                    for g in range(2):
                        for hh in range(2):
                            h = g * 2 + hh
                            P("pe", lambda e: e.matmul(OTp[ob][:, h, :], lhsT=stbb[j][:, hh, g, :], rhs=qebt[tb][:, g, tok], start=True, stop=True),
                              reads=[stbb_b[j], qebt_b[tb]], writes=[OTp_b[ob]])
                    P("dve", lambda e: e.tensor_tensor(out=osum[ob][:], in0=oPt[tb][:, :, tok], in1=OTp[ob][:], op=ALU.add), reads=[oPt_b[tb], OTp_b[ob]], writes=[osum_b[ob]])

                def s2(j):
                    tok = slice(j * 128, (j + 1) * 128)
                    ob = j % 2
                    P("act", lambda e: e.activation(out=osq[:], in_=osum[ob][:], func=AF.Square), reads=[osum_b[ob]], writes=[osq_b])
                    P("pe", lambda e: e.matmul(SSp[:].rearrange("p h c -> p (h c)"), lhsT=ones_bf[:], rhs=osq[:].rearrange("p h c -> p (h c)"), start=True, stop=True),
                      reads=[ones_b, osq_b], writes=[SSp_b])
                    P("act", lambda e: e.activation(out=grs[:], in_=SSp[:], func=AF.Ln, scale=1.0 / 128, bias=epsb[:]), reads=[SSp_b, eps_b], writes=[grs_b])
                    P("act", lambda e: e.activation(out=grs[:], in_=grs[:], func=AF.Exp, scale=-0.5), reads=[grs_b], writes=[grs_b])
                    P("dve", lambda e: e.tensor_tensor(out=osum[ob][:], in0=osum[ob][:], in1=grs[:], op=ALU.mult), reads=[osum_b[ob], grs_b], writes=[osum_b[ob]])
                    for h in range(4):
                        P("dve", lambda e: e.scalar_tensor_tensor(out=mixG[tb][:, h, tok], in0=osum[ob][:, h, :], scalar=ggs[:, h:h + 1], in1=sogt[tb][:, h, tok], op0=ALU.mult, op1=ALU.mult),
                          reads=[osum_b[ob], ggs_b, sogt_b[tb]], writes=[mixG_b[tb]])

                states(3); yield
                states(2); s1(3); yield
                states(1); s1(2); yield
                s2(3); yield
                states(0); s1(1); yield
                s2(2); yield
                s1(0); yield
                s2(1); yield
                s2(0); yield

            def Xc(si, t):
                sc = scr[si]
                tb = t % 2
                t0 = t * 512
                for j in range(4):
                    tok = slice(j * 128, (j + 1) * 128)
                    xb = (t % 2) * 4 + j
                    for c in range(2):
                        yi = ycount[0] % 4
                        ycount[0] += 1
                        for kc in range(8):
                            lhs = mixAt[tb][:, kc, tok] if kc < 4 else mixG[tb][:, kc - 4, tok]
                            P("pe", lambda e: e.matmul(Yp[yi][:], lhsT=lhs, rhs=wout[:, kc, c * 512:(c + 1) * 512], start=(kc == 0), stop=(kc == 7)),
                              reads=[mixAt_b[tb], mixG_b[tb], wout_b], writes=[Yp_b[yi]])
                        P("dve", lambda e: e.tensor_tensor(out=xc[xb][:, c * 512:(c + 1) * 512], in0=xc[xb][:, c * 512:(c + 1) * 512], in1=Yp[yi][:], op=ALU.add),
                          reads=[xc_b[xb], Yp_b[yi]], writes=[xc_b[xb]])
                        yield
                    S.dma("sp", [(sc["x1"][t0 + j * 128:t0 + (j + 1) * 128, :], xc[xb][:])], xc_b[xb], reads=[xc_b[xb]])

            def run_interleaved_c(gens):
                gens = [g for g in gens if g is not None]
                while gens:
                    for g in list(gens):
                        try:
                            next(g)
                        except StopIteration:
                            gens.remove(g)

            for si, slen in enumerate(seq_lens):
                ntile = slen // 512
                P("pool", lambda e: e.memset(stb[:], 0.0), writes=[stb_b])
                for i_ in range(4):
                    P("pool", lambda e: e.memset(stbb[i_][:], 0.0), writes=[stbb_b[i_]])
                load_tile(si, ntile - 1)
                if ntile >= 2:
                    load_tile(si, ntile - 2)
                run_interleaved_c([Yc(si, ntile - 1)])
                for t in range(ntile - 1, -1, -1):
                    run_interleaved_c([Xc(si, t), Yc(si, t - 1) if t - 1 >= 0 else None])
                    if t - 2 >= 0:
                        load_tile(si, t - 2)
            S.barrier()

        if "D" in phases:
          with contextlib.ExitStack() as pd:
            wu = sbt(pd, "wu", [128, 8, DFF], BF16); wu_b = Buf("wu")
            wd = sbt(pd, "wd", [128, NFF, D], BF16); wd_b = Buf("wd")
            g2t = sbt(pd, "g2t", [128, D], F32); g2_b = Buf("g2t")
            gft = sbt(pd, "gft", [128, D], F32); gf_b = Buf("gft")
            S.dma("pool", [(wu[:, k, :], wu_d[k * 128:(k + 1) * 128, :]) for k in range(8)], wu_b, writes=[wu_b])
            S.dma("pool", [(wd[:, k, :], wd_d[k * 128:(k + 1) * 128, :]) for k in range(NFF)], wd_b, writes=[wd_b])
            S.dma("sp", [(g2t[:], g2_d.partition_broadcast(128))], g2_b, writes=[g2_b])
            S.dma("sp", [(gft[:], gf_d.partition_broadcast(128))], gf_b, writes=[gf_b])
            NX = 6
            xd = [sbt(pd, "xd%d" % i, [128, D], F32) for i in range(NX)]; xd_b = [Buf("xd%d" % i) for i in range(NX)]
            xdst_b = [Buf("xdst%d" % i) for i in range(NX)]
            hn2 = [sbt(pd, "hn2_%d" % i, [128, D], BF16) for i in range(2)]; hn2_b = [Buf("hn2_0"), Buf("hn2_1")]
            hn2T = [sbt(pd, "hn2T%d" % i, [128, 8, 256], BF16) for i in range(2)]; hn2T_b = [Buf("hn2T0"), Buf("hn2T1")]
            hT = sbt(pd, "hT", [128, NFF, 256], BF16); hT_b = Buf("hT")
            sg = [sbt(pd, "sg%d" % i, [128, 256], F32) for i in range(2)]; sg_b = [Buf("sg0"), Buf("sg1")]
            ss2 = sbt(pd, "ss2", [128, 6], F32); ss2_b = [Buf("ss2_%d" % i) for i in range(6)]
            rs2 = sbt(pd, "rs2", [128, 6], F32); rs2_b = [Buf("rs2_%d" % i) for i in range(6)]
            junk2 = sbt(pd, "junk2", [128, D], BF16); junk2_b = Buf("junk2")
            pT2 = pst(pd, "pT2", [128, 1024], BF16); pT2_b = Buf("pT2")
            GU = [pst(pd, "GU%d" % i, [128, 2, 256], F32) for i in range(3)]; GU_b = [Buf("GU%d" % i) for i in range(3)]
            Yd = [pst(pd, "Yd%d" % i, [128, 512], F32) for i in range(4)]; Yd_b = [Buf("Yd%d" % i) for i in range(4)]
            tiles = [(si, t) for si in range(nseq) for t in range(d_lens[si] // 256)]
            idxs = [sbt(pd, "ridxs%d" % i, [128, d_lens[i] // 128], mybir.dt.int32) for i in range(nseq)]; idxs_b = [Buf("ridx%d" % i) for i in range(nseq)]
            for i in range(nseq):
                S.dma("sp", [(idxs[i][:], ridx_d[i])], idxs_b[i], writes=[idxs_b[i]])
            xcount = [0]
            xslot = {}

            def load_x1(ti):
                si, t = tiles[ti]
                for j in range(2):
                    xb = xcount[0] % NX
                    xcount[0] += 1
                    xslot[(ti, j)] = xb
                    k_ = t * 2 + j
                    S.dma_custom("pool", (lambda e, xb=xb, si=si, k_=k_: e.indirect_dma_start(
                        out=xd[xb][:, :], out_offset=None, in_=scr[si]["x1"][:, :],
                        in_offset=bass.IndirectOffsetOnAxis(ap=idxs[si][:, k_:k_ + 1], axis=0))), xd_b[xb], reads=[idxs_b[si]], writes=[xd_b[xb]])

            load_x1(0)
            gcount = [0]
            ycount = [0]

            def norm2_part(ti, j):
                xb = xslot[(ti, j)]
                sj = (ti * 2 + j) % 4
                P("act", lambda e: e.activation(out=junk2[:], in_=xd[xb][:], func=AF.Square, accum_out=ss2[:, sj:sj + 1]), reads=[xd_b[xb]], writes=[junk2_b, ss2_b[sj]])
                P("act", lambda e: e.activation(out=rs2[:, sj:sj + 1], in_=ss2[:, sj:sj + 1], func=AF.Ln, scale=1.0 / D, bias=epsb[:]), reads=[ss2_b[sj], eps_b], writes=[rs2_b[sj]])
                P("act", lambda e: e.activation(out=rs2[:, sj:sj + 1], in_=rs2[:, sj:sj + 1], func=AF.Exp, scale=-0.5), reads=[rs2_b[sj]], writes=[rs2_b[sj]])
                P("dve", lambda e: e.scalar_tensor_tensor(out=hn2[j][:], in0=xd[xb][:], scalar=rs2[:, sj:sj + 1], in1=g2t[:], op0=ALU.mult, op1=ALU.mult),
                  reads=[xd_b[xb], rs2_b[sj], g2_b], writes=[hn2_b[j]])

            def transp_part(ti, j):
                tb = ti % 2
                for k in range(8):
                    P("pe", lambda e: e.transpose(out=pT2[:, k * 128:(k + 1) * 128], in_=hn2[j][:, k * 128:(k + 1) * 128], identity=ident[:]), reads=[hn2_b[j], ident_b], writes=[pT2_b])
                P("dve", lambda e: e.tensor_copy(out=hn2T[tb][:, :, j * 128:(j + 1) * 128], in_=pT2[:].rearrange("p (k t) -> p k t", k=8)), reads=[pT2_b], writes=[hn2T_b[tb]])

            for j in range(2):
                norm2_part(0, j)
                transp_part(0, j)
            for ti, (si, t) in enumerate(tiles):
                if ti + 1 < len(tiles):
                    load_x1(ti + 1)
                tb = ti % 2
                nxt = ti + 1 < len(tiles)
                for f in range(NFF):
                    gi = gcount[0] % 3
                    gcount[0] += 1
                    for k in range(8):
                        P("pe", lambda e: e.matmul(GU[gi][:, 0, :], lhsT=wg[:, k, f * 128:(f + 1) * 128], rhs=hn2T[tb][:, k, :], start=(k == 0), stop=(k == 7)), reads=[wg_b, hn2T_b[tb]], writes=[GU_b[gi]])
                    for k in range(8):
                        P("pe", lambda e: e.matmul(GU[gi][:, 1, :], lhsT=wu[:, k, f * 128:(f + 1) * 128], rhs=hn2T[tb][:, k, :], start=(k == 0), stop=(k == 7)), reads=[wu_b, hn2T_b[tb]], writes=[GU_b[gi]])
                    sb_ = f % 2
                    P("act", lambda e: e.activation(out=sg[sb_][:], in_=GU[gi][:, 0, :], func=AF.Silu), reads=[GU_b[gi]], writes=[sg_b[sb_]])
                    P("dve", lambda e: e.tensor_tensor(out=hT[:, f, :], in0=sg[sb_][:], in1=GU[gi][:, 1, :], op=ALU.mult), reads=[sg_b[sb_], GU_b[gi]], writes=[hT_b])
                    if nxt and f == 6:
                        norm2_part(ti + 1, 0)
                        norm2_part(ti + 1, 1)
                    if nxt and f == 12:
                        transp_part(ti + 1, 0)
                    if nxt and f == 16:
                        transp_part(ti + 1, 1)
                for j in range(2):
                    xb = xslot[(ti, j)]
                    sj = (ti * 2 + j) % 4
                    tok = slice(j * 128, (j + 1) * 128)
                    for c in range(2):
                        yi = ycount[0] % 4
                        ycount[0] += 1
                        for f in range(NFF):
                            P("pe", lambda e: e.matmul(Yd[yi][:], lhsT=hT[:, f, tok], rhs=wd[:, f, c * 512:(c + 1) * 512], start=(f == 0), stop=(f == NFF - 1)), reads=[hT_b, wd_b], writes=[Yd_b[yi]])
                        P("dve", lambda e: e.tensor_tensor(out=xd[xb][:, c * 512:(c + 1) * 512], in0=xd[xb][:, c * 512:(c + 1) * 512], in1=Yd[yi][:], op=ALU.add), reads=[xd_b[xb], Yd_b[yi]], writes=[xd_b[xb]])
                    sjf = 4 + j
                    P("act", lambda e: e.activation(out=junk2[:], in_=xd[xb][:], func=AF.Square, accum_out=ss2[:, sjf:sjf + 1]), reads=[xd_b[xb]], writes=[junk2_b, ss2_b[sjf]])
                    P("act", lambda e: e.activation(out=rs2[:, sjf:sjf + 1], in_=ss2[:, sjf:sjf + 1], func=AF.Ln, scale=1.0 / D, bias=epsb[:]), reads=[ss2_b[sjf], eps_b], writes=[rs2_b[sjf]])
                    P("act", lambda e: e.activation(out=rs2[:, sjf:sjf + 1], in_=rs2[:, sjf:sjf + 1], func=AF.Exp, scale=-0.5), reads=[rs2_b[sjf]], writes=[rs2_b[sjf]])
                    P("dve", lambda e: e.scalar_tensor_tensor(out=xd[xb][:], in0=xd[xb][:], scalar=rs2[:, sjf:sjf + 1], in1=gft[:], op0=ALU.mult, op1=ALU.mult),
                      reads=[xd_b[xb], rs2_b[sjf], gf_b], writes=[xd_b[xb]])
                    r0 = t * 256 + j * 128
                    S.dma("sp", [(ys[si][r0:r0 + 128, :], xd[xb][:])], xdst_b[xb], reads=[xd_b[xb]])
            S.barrier()

        S.emit()
    return nc


def permute_w_in(w_in):
    w = np.asarray(w_in).reshape(D, 3104)
    qa, ka, va = w[:, 0:512], w[:, 512:1024], w[:, 1024:1536]
    qg, kg, vg, og = w[:, 1536:1792], w[:, 1792:2048], w[:, 2048:2560], w[:, 2560:3072]
    rr = w[:, 3072:3104]
    cols = []
    for m in (qa, ka):
        m4 = m.reshape(D, 8, 2, 32)
        for half_heads in (slice(0, 4), slice(4, 8)):
            cols.append(m4[:, half_heads, 0, :].reshape(D, 128))
            cols.append(m4[:, half_heads, 1, :].reshape(D, 128))
    cols += [qg, kg, og, rr]
    cols += [va, vg, kg]
    out = np.concatenate(cols, axis=1)
    return np.ascontiguousarray(out, dtype=np.float32)


_ROPE_THETA = 10000.0


def _rope_tables(s):
    inv = (_ROPE_THETA ** (-np.arange(0, 64, 2, dtype=np.float32) / np.float32(64))).astype(np.float32)
    ang = (np.arange(s, dtype=np.float32)[:, None] * inv[None, :]).astype(np.float32)
    c = np.cos(ang).astype(np.float32).T
    sn = np.sin(ang).astype(np.float32).T
    return np.ascontiguousarray(np.tile(c, (4, 1))), np.ascontiguousarray(np.tile(sn, (4, 1)))


_PROG = {}


def kernel(x_prompt, x_sample, norm1_g, w_in, gate_up_fwd, gate_bias_fwd, gate_up_bwd, gate_bias_bwd,
           gla_norm_g, w_out, norm2_g, w_gate, w_up, w_down, final_norm_g):
    f32 = lambda a: np.ascontiguousarray(np.asarray(a, dtype=np.float32))
    x_prompt = f32(x_prompt)
    x_sample = f32(x_sample)
    nb_p, s_p, _ = x_prompt.shape
    nb_s, s_s, _ = x_sample.shape
    n = 8
    seq_lens = (s_s, s_p)
    split = (n == 2 * nb_p and n == nb_s)
    d_lens = (s_s, s_p // 2) if split else (s_s, s_p)
    if seq_lens not in _PROG:
        _PROG[seq_lens] = build_program(list(seq_lens), d_lens=d_lens)
    nc = _PROG[seq_lens]
    cos0, sin0 = _rope_tables(s_s)
    cos1, sin1 = _rope_tables(s_p)
    shared = {
        "cos0": cos0, "sin0": sin0, "cos1": cos1, "sin1": sin1,
        "w_in_p": permute_w_in(f32(w_in)[0]),
        "w_out": f32(w_out)[0], "w_gate": f32(w_gate)[0], "w_up": f32(w_up)[0], "w_down": f32(w_down)[0],
        "gate_up_fwd": f32(gate_up_fwd)[0], "gate_up_bwd": f32(gate_up_bwd)[0],
        "gate_bias_fwd": f32(gate_bias_fwd).reshape(1, 256), "gate_bias_bwd": f32(gate_bias_bwd).reshape(1, 256),
        "norm1_g": f32(norm1_g).reshape(1, D), "norm2_g": f32(norm2_g).reshape(1, D),
        "final_norm_g": f32(final_norm_g).reshape(1, D),
        "gla_norm_g": np.ascontiguousarray(f32(gla_norm_g).reshape(4, 128).T),
    }
    in_maps = []
    for c in range(n):
        m = dict(shared)
        m["x0"] = x_sample[c % nb_s]
        m["x1"] = x_prompt[c % nb_p]
        half = (c // nb_p) if split else 0
        m["ridx0"] = np.ascontiguousarray(np.arange(d_lens[0], dtype=np.int32).reshape(-1, 128).T)
        m["ridx1"] = np.ascontiguousarray((half * d_lens[1] + np.arange(d_lens[1], dtype=np.int32)).reshape(-1, 128).T)
        in_maps.append(m)
    res = run_bass_kernel_spmd(nc, in_maps, core_ids=list(range(n)))
    y_sample = np.stack([np.asarray(res.results[c]["y0"], dtype=np.float32) for c in range(nb_s)], axis=0)
    if split:
        y_prompt = np.stack([np.concatenate([np.asarray(res.results[c]["y1"], dtype=np.float32),
                                             np.asarray(res.results[c + nb_p]["y1"], dtype=np.float32)], axis=0) for c in range(nb_p)], axis=0)
    else:
        y_prompt = np.stack([np.asarray(res.results[c]["y1"], dtype=np.float32) for c in range(nb_p)], axis=0)
    return (y_prompt, y_sample)
```

```python
import contextlib
import numpy as np
import concourse.bass as bass
import concourse.mybir as mybir
from concourse.bass_utils import run_bass_kernel_spmd

F32 = mybir.dt.float32
BF16 = mybir.dt.bfloat16
AF = mybir.ActivationFunctionType
ALU = mybir.AluOpType

ENGS = ("pe", "act", "dve", "pool", "sp")
D = 1024
DFF = 2816
NFF = DFF // 128
NWIN = 3360
PADR = 1024
EPS = 1e-6


class Buf:
    __slots__ = ("name", "w", "r", "dsem")

    def __init__(self, name):
        self.name = name
        self.w = {}
        self.r = {}
        self.dsem = None


import os
KSTOP = int(os.environ.get("KSTOP", "0"))


class _Stop(Exception):
    pass


def sl(start, n, step):
    return slice(start, start + (n - 1) * step + 1, step)


def stop_at(n):
    if KSTOP == n:
        raise _Stop()


class _Rec:
    def __getattr__(self, name):
        def f(*a, **k):
            self.call = (name, a, k)
            return self
        return f


class Sched:
    def __init__(self, nc, stack, n_dsem=56):
        self.nc = nc
        self.q = {e: [] for e in ENGS}
        self.cnt = {e: 0 for e in ENGS}
        self.sems = {e: stack.enter_context(nc.semaphore("prog_" + e)) for e in ENGS}
        self.seen = {e: {} for e in ENGS}
        self.dtotal = {}
        self.free_dsem = []
        for i in range(n_dsem):
            k = "d%d" % i
            self.sems[k] = stack.enter_context(nc.semaphore(k))
            self.dtotal[k] = 0
            self.free_dsem.append(k)
        self.phase_bufs = []
        self.fresh_dsem = []
        for i in range(14):
            k = "dsw%d" % i
            self.sems[k] = stack.enter_context(nc.semaphore(k))
            self.dtotal[k] = 0
            self.fresh_dsem.append(k)

    def _need(self, eng, key, val, waits):
        if val <= 0 or self.seen[eng].get(key, 0) >= val:
            return
        self.seen[eng][key] = val
        waits[key] = max(waits.get(key, 0), val)

    def _deps(self, eng, reads, writes, waits, same_engine_raw=True):
        for b in reads:
            for k, v in b.w.items():
                if k == eng and (eng == "pe" or not same_engine_raw):
                    continue
                self._need(eng, k, v, waits)
        for b in writes:
            for k, v in b.w.items():
                if k == eng and (eng == "pe" or not same_engine_raw):
                    continue
                self._need(eng, k, v, waits)
            for k, v in b.r.items():
                if k == eng and (eng == "pe" or not same_engine_raw):
                    continue
                self._need(eng, k, v, waits)

    def op(self, eng, fn, reads=(), writes=()):
        rec = _Rec()
        fn(rec)
        name, a, k = rec.call
        fn = (lambda e, name=name, a=a, k=k: getattr(e, name)(*a, **k))
        waits = {}
        self._deps(eng, reads, writes, waits)
        self.cnt[eng] += 1
        seq = self.cnt[eng]
        for b in reads:
            b.r[eng] = seq
        for b in writes:
            b.w[eng] = seq
            b.r = {}
        self.q[eng].append((list(waits.items()), fn, (eng, 1)))

    def dma(self, eng, pairs, sbuf, reads=(), writes=()):
        if sbuf.dsem is None:
            if eng == "pool":
                sbuf.dsem = self.fresh_dsem.pop()
            else:
                sbuf.dsem = self.free_dsem.pop()
                self.phase_bufs.append(sbuf)
        key = sbuf.dsem
        waits = {}
        self._need(eng, key, self.dtotal[key], waits)
        self._deps(eng, reads, writes, waits, same_engine_raw=False)
        self.dtotal[key] += 16 * len(pairs)
        tot = self.dtotal[key]
        for b in reads:
            b.r[key] = tot
        for b in writes:
            b.w[key] = tot
            b.r = {}
        first = True
        for (o, i) in pairs:
            w = list(waits.items()) if first else []
            first = False
            self.q[eng].append((w, (lambda e, o=o, i=i: e.dma_start(out=o, in_=i)), (key, 16)))

    def dma_custom(self, eng, fn, sbuf, reads=(), writes=()):
        if sbuf.dsem is None:
            sbuf.dsem = self.fresh_dsem.pop() if eng == "pool" else self.free_dsem.pop()
            if eng != "pool":
                self.phase_bufs.append(sbuf)
        key = sbuf.dsem
        waits = {}
        self._need(eng, key, self.dtotal[key], waits)
        self._deps(eng, reads, writes, waits, same_engine_raw=False)
        self.dtotal[key] += 16
        tot = self.dtotal[key]
        for b in reads:
            b.r[key] = tot
        for b in writes:
            b.w[key] = tot
            b.r = {}
        self.q[eng].append((list(waits.items()), fn, (key, 16)))

    def barrier(self):
        tgt = dict(self.cnt)
        tgt.update(self.dtotal)
        for e in ENGS:
            waits = {}
            for k, v in tgt.items():
                if k != e:
                    self._need(e, k, v, waits)
            self.cnt[e] += 1
            self.q[e].append((list(waits.items()), (lambda eo: eo.nop()), (e, 1)))
        for b in self.phase_bufs:
            if b.dsem is not None:
                self.free_dsem.append(b.dsem)
                b.dsem = None
        self.phase_bufs = []

    def emit(self):
        nc = self.nc
        engobj = {"pe": "tensor", "act": "scalar", "dve": "vector", "pool": "gpsimd", "sp": "sync"}
        final = [(k, v) for k, v in self.dtotal.items() if v > 0]
        with nc.Block() as block:
            for e in ENGS:
                items = self.q[e]

                def body(eobj, items=items, e=e):
                    for waits, fn, (ikey, inc) in items:
                        for k, v in waits:
                            eobj.wait_ge(self.sems[k], v)
                        fn(eobj).then_inc(self.sems[ikey], inc)
                    if e == "sp":
                        for k, v in final:
                            eobj.wait_ge(self.sems[k], v)

                getattr(block, engobj[e])(body)


def build_program(seq_lens, phases="ABCD", debug=False, d_lens=None):
    d_lens = list(d_lens) if d_lens is not None else list(seq_lens)
    nc = bass.Bass("TRN2", target_bir_lowering=False)
    nseq = len(seq_lens)
    dt_in = lambda name, shape, dt=F32: nc.dram_tensor(name, shape, dt, kind="ExternalInput").ap()
    dt_out = lambda name, shape, dt=F32: nc.dram_tensor(name, shape, dt, kind="ExternalOutput").ap()
    dt_tmp = lambda name, shape, dt: nc.dram_tensor(name, shape, dt, kind=("ExternalOutput" if debug else "Internal")).ap()

    xs = [dt_in("x%d" % i, [s, D]) for i, s in enumerate(seq_lens)]
    ys = [dt_out("y%d" % i, [s, D]) for i, s in enumerate(d_lens)]
    ridx_d = [nc.dram_tensor("ridx%d" % i, [128, s // 128], mybir.dt.int32, kind="ExternalInput").ap() for i, s in enumerate(d_lens)]
    cosd = [dt_in("cos%d" % i, [128, s]) for i, s in enumerate(seq_lens)]
    sind = [dt_in("sin%d" % i, [128, s]) for i, s in enumerate(seq_lens)]
    win_d = dt_in("w_in_p", [D, NWIN])
    wout_d = dt_in("w_out", [D, D])
    wg_d = dt_in("w_gate", [D, DFF])
    wu_d = dt_in("w_up", [D, DFF])
    wd_d = dt_in("w_down", [DFF, D])
    upf_d = dt_in("gate_up_fwd", [16, 256])
    upb_d = dt_in("gate_up_bwd", [16, 256])
    bf_d = dt_in("gate_bias_fwd", [1, 256])
    bb_d = dt_in("gate_bias_bwd", [1, 256])
    g1_d = dt_in("norm1_g", [1, D])
    g2_d = dt_in("norm2_g", [1, D])
    gf_d = dt_in("final_norm_g", [1, D])
    gg_d = dt_in("gla_norm_g", [128, 4])

    scr = []
    for i, s in enumerate(seq_lens):
        scr.append(dict(
            qT=dt_tmp("qT%d" % i, [8, 64, s], BF16),
            kT=dt_tmp("kT%d" % i, [8, 64, s], BF16),
            v=dt_tmp("v%d" % i, [s + 2 * PADR, 8 * 128], BF16),
            oP=dt_tmp("oP%d" % i, [4, 128, s], F32),
            qeb=dt_tmp("qeb%d" % i, [2, 128, s], BF16),
            kvb=dt_tmp("kvb%d" % i, [s // 512, 128, 4 * 2 * 128], F32),
            sog=dt_tmp("sog%d" % i, [4, 128, s], BF16),
            mixA=dt_tmp("mixA%d" % i, [4, 128, s], BF16),
            x1=dt_tmp("x1_%d" % i, [s, D], F32),
        ))

    with contextlib.ExitStack() as st:
        S = Sched(nc, st)
        G = st
        G0 = G

        def sbt(stack, name, shape, dt):
            return stack.enter_context(nc.sbuf_tensor(name, shape, dt))

        def pst(stack, name, shape, dt=F32):
            return stack.enter_context(nc.psum_tensor(name, shape, dt))


        ident = sbt(G, "ident", [128, 128], BF16); ident_b = Buf("ident")
        cf = sbt(G, "cf", [128, 5, 128], F32); cf_b = Buf("cf")
        ones_bf = sbt(G, "ones_bf", [128, 128], BF16); ones_b = Buf("ones")
        gmask = sbt(G, "gmask", [128, 2, 2, 128], F32); gmask_b = Buf("gmask")
        amask = sbt(G, "amask", [128, 256], BF16); amask_b = Buf("amask")
        epsb = sbt(G, "epsb", [128, 1], F32); eps_b = Buf("eps")
        oneb = sbt(G, "oneb", [128, 1], F32); one_b = Buf("one")
        upbm = sbt(G, "upbm", [33, 512], BF16); upbm_b = Buf("upbm")
        ggs = sbt(G, "ggs", [128, 4], F32); ggs_b = Buf("ggs")
        decb = [sbt(G, "decb%d" % i, [128, s // 128, 2], F32) for i, s in enumerate(seq_lens)]
        decb_b = [Buf("decb%d" % i) for i in range(nseq)]
        amask_f = sbt(G0, "amask_f", [128, 256], F32)
        upst = sbt(G0, "upst", [33, 512], F32); upst_b = Buf("upst")
        zero_bf = sbt(G0, "zero_bf", [128, 256], BF16); zero_b = Buf("zero")

        P = S.op
        P("pool", lambda e: e.memset(cf[:], 1.0), writes=[cf_b])
        P("pool", lambda e: e.affine_select(out=cf[:, 0, :], in_=cf[:, 0, :], pattern=[[-1, 128]], compare_op=ALU.is_equal, fill=0.0, base=0, channel_multiplier=1), reads=[cf_b], writes=[cf_b])
        P("pool", lambda e: e.affine_select(out=cf[:, 1, :], in_=cf[:, 1, :], pattern=[[1, 128]], compare_op=ALU.is_ge, fill=0.0, base=0, channel_multiplier=-1), reads=[cf_b], writes=[cf_b])
        P("pool", lambda e: e.affine_select(out=cf[:, 2, :], in_=cf[:, 2, :], pattern=[[1, 128]], compare_op=ALU.is_gt, fill=0.0, base=0, channel_multiplier=-1), reads=[cf_b], writes=[cf_b])
        P("pool", lambda e: e.affine_select(out=cf[:, 3, :], in_=cf[:, 3, :], pattern=[[-1, 128]], compare_op=ALU.is_ge, fill=0.0, base=0, channel_multiplier=1), reads=[cf_b], writes=[cf_b])
        P("pool", lambda e: e.affine_select(out=cf[:, 4, :], in_=cf[:, 4, :], pattern=[[-1, 128]], compare_op=ALU.is_gt, fill=0.0, base=0, channel_multiplier=1), reads=[cf_b], writes=[cf_b])
        P("dve", lambda e: e.tensor_copy(out=ident[:], in_=cf[:, 0, :]), reads=[cf_b], writes=[ident_b])
        for hh_ in range(2):
            P("dve", lambda e: e.tensor_copy(out=gmask[:, hh_, 0, :], in_=cf[:, 1, :]), reads=[cf_b], writes=[gmask_b])
            P("dve", lambda e: e.tensor_copy(out=gmask[:, hh_, 1, :], in_=cf[:, 4, :]), reads=[cf_b], writes=[gmask_b])
        P("pool", lambda e: e.memset(ones_bf[:], 1.0), writes=[ones_b])
        P("pool", lambda e: e.memset(zero_bf[:], 0.0), writes=[zero_b])
        P("pool", lambda e: e.memset(epsb[:], EPS), writes=[eps_b])
        P("pool", lambda e: e.memset(oneb[:], 1.0), writes=[one_b])
        P("pool", lambda e: e.memset(amask_f[:], 1.0), writes=[amask_b])
        P("pool", lambda e: e.affine_select(out=amask_f[:], in_=amask_f[:], pattern=[[1, 256]], compare_op=ALU.is_ge, fill=0.0, base=0, channel_multiplier=-1), reads=[amask_b], writes=[amask_b])
        P("pool", lambda e: e.affine_select(out=amask_f[:], in_=amask_f[:], pattern=[[-1, 256]], compare_op=ALU.is_ge, fill=0.0, base=128, channel_multiplier=1), reads=[amask_b], writes=[amask_b])
        P("dve", lambda e: e.tensor_copy(out=amask[:], in_=amask_f[:]), reads=[amask_b], writes=[amask_b])
        P("pool", lambda e: e.memset(upst[:], 0.0), writes=[upst_b])
        S.dma("sp", [(upst[0:16, 0:256], upf_d), (upst[16:32, 256:512], upb_d),
                     (upst[32:33, 0:256], bf_d), (upst[32:33, 256:512], bb_d)], upst_b, writes=[upst_b])
        P("dve", lambda e: e.tensor_copy(out=upbm[:], in_=upst[:]), reads=[upst_b], writes=[upbm_b])
        S.dma("sp", [(ggs[:], gg_d)], ggs_b, writes=[ggs_b])

        for i, s in enumerate(seq_lens):
            vd = scr[i]["v"]
            S.dma("sp", [(vd[k * 128:(k + 1) * 128, c_ * 256:(c_ + 1) * 256], zero_bf[:]) for k in range(PADR // 128) for c_ in range(4)] +
                        [(vd[PADR + s + k * 128:PADR + s + (k + 1) * 128, c_ * 256:(c_ + 1) * 256], zero_bf[:]) for k in range(PADR // 128) for c_ in range(4)],
                  zero_b, reads=[zero_b])


        if "A" in phases:
          with contextlib.ExitStack() as pa:
            win = sbt(pa, "win", [128, 8, NWIN], BF16); win_b = Buf("win")
            g1t = sbt(pa, "g1t", [128, D], F32); g1_b = Buf("g1t")
            S.dma("pool", [(win[:, k, :], win_d[k * 128:(k + 1) * 128, :]) for k in range(8)], win_b, writes=[win_b])
            S.dma("sp", [(g1t[:], g1_d.partition_broadcast(128))], g1_b, writes=[g1_b])

            xt = [sbt(pa, "xt%d" % i, [128, D], F32) for i in range(3)]; xt_b = [Buf("xt%d" % i) for i in range(3)]
            ss = sbt(pa, "ss", [128, 2], F32); ss_b = [Buf("ss0"), Buf("ss1")]
            rstd = sbt(pa, "rstd", [128, 2], F32); rstd_b = [Buf("rstd0"), Buf("rstd1")]
            hn = [sbt(pa, "hn%d" % i, [128, D], BF16) for i in range(2)]; hn_b = [Buf("hn0"), Buf("hn1")]
            hnT = [sbt(pa, "hnT%d" % i, [128, 8, 512], BF16) for i in range(2)]; hnT_b = [Buf("hnT0"), Buf("hnT1")]
            cst = [sbt(pa, "cst%d" % i, [128, 2, 512], F32) for i in range(2)]; cst_b = [Buf("cst0"), Buf("cst1")]
            rps = [sbt(pa, "rp%d" % i, [128, 4, 512], F32) for i in range(2)]; rps_b = [Buf("rp0"), Buf("rp1")]
            ro = [sbt(pa, "ro%d" % i, [128, 2, 512], BF16) for i in range(2)]; ro_b = [Buf("ro0"), Buf("ro1")]
            qgs = [sbt(pa, "qgs%d" % i, [128, 2, 512], F32) for i in range(2)]; qgs_b = [Buf("qgs0"), Buf("qgs1")]
            kgs = [sbt(pa, "kgs%d" % i, [128, 2, 512], F32) for i in range(2)]; kgs_b = [Buf("kgs0"), Buf("kgs1")]
            sogs = sbt(pa, "sogs", [128, 4, 512], BF16); sogs_b = Buf("sogs")
            rT = [sbt(pa, "rT%d" % i, [33, 512], BF16) for i in range(2)]; rT_b = [Buf("rT0"), Buf("rT1")]
            vga = [sbt(pa, "vga%d" % i, [128, 4, 512], BF16) for i in range(2)]; vga_b = [[Buf("vga%d_%d" % (i, j)) for j in range(4)] for i in range(2)]
            kgt = [sbt(pa, "kgt%d" % i, [128, 4, 256], F32) for i in range(2)]; kgt_b = [[Buf("kgt%d_%d" % (i, j)) for j in range(4)] for i in range(2)]
            vst = [sbt(pa, "vst%d" % i, [128, 8, 128], BF16) for i in range(2)]; vst_b = [Buf("vst0"), Buf("vst1")]
            Lt = sbt(pa, "Lt", [128, 512], F32); Lt_b = Buf("Lt")
            ek = sbt(pa, "ek", [128, 2, 256], F32); ek_b = Buf("ek")
            kend = [sbt(pa, "kend%d" % i, [128, 2, 256], BF16) for i in range(2)]; kend_b = [Buf("kend0"), Buf("kend1")]
            epm = sbt(pa, "epm", [128, 2, 4, 128], F32); epm_b = Buf("epm")
            qe = [sbt(pa, "qe%d" % i, [128, 2, 2, 512], BF16) for i in range(2)]; qe_b = [Buf("qe0"), Buf("qe1")]
            ke = [sbt(pa, "ke%d" % i, [128, 2, 2, 2, 128], BF16) for i in range(2)]; ke_b = [Buf("ke0"), Buf("ke1")]
            ats = sbt(pa, "ats", [128, 2, 2, 128], BF16); ats_b = Buf("ats")
            oPs = sbt(pa, "oPs", [128, 4, 512], F32); oPs_b = Buf("oPs")
            kvbs = sbt(pa, "kvbs", [128, 4, 2, 128], F32); kvbs_b = Buf("kvbs")
            stf = sbt(pa, "stf", [128, 2, 128], F32); stf_b = Buf("stf")
            stfb = sbt(pa, "stfb", [128, 2, 2, 128], BF16); stfb_b = Buf("stfb")
            decf = [sbt(pa, "decf%d" % i, [128, 2], F32) for i in range(2)]; decf_b = [Buf("decf0"), Buf("decf1")]

            for i in range(2):
                P("pool", lambda e: e.memset(rT[i][:], 1.0), writes=[rT_b[i]])
                P("pool", lambda e: e.memset(vst[i][:], 1.0), writes=[vst_b[i]])
            for i in range(2):
                P("pool", lambda e: e.memset(ke[i][:], 0.0), writes=[ke_b[i]])

            PB = [None] + [pst(pa, "pa%d" % i, [128, 512], F32) for i in range(1, 8)]
            PBb = [Buf("pb%d" % i) for i in range(8)]
            pT16t = pst(pa, "paT", [128, 1024], BF16); pT_b = PBb[0]
            pT16 = pT16t[:]
            acc = [PB[1], PB[2], PB[4]]; acc_b = [PBb[1], PBb[2], PBb[4]]
            acc_i = [0]

            def next_acc():
                i = acc_i[0] % 3
                acc_i[0] += 1
                return acc[i], acc_b[i]
            g1p, g1p_b = PB[3], PBb[3]
            g2p, g2p_b = PB[3], PBb[3]
            g3p, g3p_b = PB[5], PBb[5]
            g4p, g4p_b = PB[6], PBb[6]
            g5p, g5p_b = PB[7], PBb[7]

            xcount = [0]
            rpi = [0]
            vcnt = [0]

            def X(si, t):
                sc = scr[si]
                slen = seq_lens[si]
                ntile = slen // 512
                tb = t % 2
                t0 = t * 512
                def load_cst(tt):
                    S.dma("sp", [(cst[tt % 2][:, 0, :], cosd[si][:, tt * 512:tt * 512 + 512]), (cst[tt % 2][:, 1, :], sind[si][:, tt * 512:tt * 512 + 512])],
                          cst_b[tt % 2], writes=[cst_b[tt % 2]])

                if t == 0:
                    load_cst(0)
                nxt = t + 1 < ntile
                sched = {}
                if nxt:
                    for j in range(3):
                        load_xt(si, t + 1, j)
                    sched = {2: ("n", 0), 4: ("t", 0), 5: ("l", 3), 7: ("n", 1), 9: ("t", 1), 11: ("n", 2), 13: ("t", 2), 15: ("n", 3), 17: ("t", 3)}
                stepc = [0]

                def tick():
                    a_ = sched.get(stepc[0])
                    stepc[0] += 1
                    if a_ is None:
                        return
                    if a_[0] == "l":
                        load_xt(si, t + 1, a_[1])
                    elif a_[0] == "n":
                        norm_ops(si, t + 1, a_[1])
                    else:
                        transp_ops(t + 1, a_[1])

                def fm_group(gcol, ncols=128):
                    a, ab = next_acc()
                    for k in range(8):
                        P("pe", lambda e: e.matmul(a[0:ncols, :], lhsT=win[:, k, gcol:gcol + ncols], rhs=hnT[tb][:, k, :], start=(k == 0), stop=(k == 7)),
                          reads=[win_b, hnT_b[tb]], writes=[ab])
                    return a, ab

                for qk in range(2):
                    dst = sc["qT"] if qk == 0 else sc["kT"]
                    for half in range(2):
                        gbase = (qk * 4 + half * 2) * 128
                        rp, rp_b = rps[rpi[0] % 2], rps_b[rpi[0] % 2]
                        rb = rpi[0] % 2
                        rpi[0] += 1
                        a1, a1b = fm_group(gbase)
                        P("act", lambda e: e.activation(out=rp[:, 0, :], in_=a1[:], func=AF.Copy), reads=[a1b], writes=[rp_b])
                        tick()
                        yield
                        a2, a2b = fm_group(gbase + 128)
                        P("act", lambda e: e.activation(out=rp[:, 1, :], in_=a2[:], func=AF.Copy), reads=[a2b], writes=[rp_b])
                        cs, sn = cst[tb][:, 0, :], cst[tb][:, 1, :]
                        P("dve", lambda e: e.tensor_tensor(out=rp[:, 2, :], in0=rp[:, 0, :], in1=cs, op=ALU.mult), reads=[rp_b, cst_b[tb]], writes=[rp_b])
                        P("pool", lambda e: e.tensor_tensor(out=rp[:, 3, :], in0=rp[:, 1, :], in1=sn, op=ALU.mult), reads=[rp_b, cst_b[tb]], writes=[rp_b])
                        P("dve", lambda e: e.tensor_tensor(out=ro[rb][:, 0, :], in0=rp[:, 2, :], in1=rp[:, 3, :], op=ALU.subtract), reads=[rp_b], writes=[ro_b[rb]])
                        P("dve", lambda e: e.tensor_tensor(out=rp[:, 2, :], in0=rp[:, 1, :], in1=cs, op=ALU.mult), reads=[rp_b, cst_b[tb]], writes=[rp_b])
                        P("pool", lambda e: e.tensor_tensor(out=rp[:, 3, :], in0=rp[:, 0, :], in1=sn, op=ALU.mult), reads=[rp_b, cst_b[tb]], writes=[rp_b])
                        P("dve", lambda e: e.tensor_tensor(out=ro[rb][:, 1, :], in0=rp[:, 2, :], in1=rp[:, 3, :], op=ALU.add), reads=[rp_b], writes=[ro_b[rb]])
                        pairs = []
                        for hq in range(4):
                            h = half * 4 + hq
                            pairs.append((dst[h, 0:32, t0:t0 + 512], ro[rb][hq * 32:(hq + 1) * 32, 0, :]))
                            pairs.append((dst[h, 32:64, t0:t0 + 512], ro[rb][hq * 32:(hq + 1) * 32, 1, :]))
                        S.dma("sp", pairs, ro_b[rb], reads=[ro_b[rb]])
                        tick()
                        yield
                if nxt:
                    load_cst(t + 1)
                for g in range(2):
                    a, ab = fm_group((8 + g) * 128)
                    P("act", lambda e: e.activation(out=qgs[tb][:, g, :], in_=a[:], func=AF.Copy, scale=0.125), reads=[ab], writes=[qgs_b[tb]])
                    tick()
                    yield
                for g in range(2):
                    a, ab = fm_group((10 + g) * 128)
                    P("act", lambda e: e.activation(out=kgs[tb][:, g, :], in_=a[:], func=AF.Copy), reads=[ab], writes=[kgs_b[tb]])
                    tick()
                    yield
                for h in range(4):
                    a, ab = fm_group((12 + h) * 128)
                    P("act", lambda e: e.activation(out=sogs[:, h, :], in_=a[:], func=AF.Silu), reads=[ab], writes=[sogs_b])
                    tick()
                S.dma("sp", [(sc["sog"][h, :, t0:t0 + 512], sogs[:, h, :]) for h in range(4)], sogs_b, reads=[sogs_b])
                a, ab = fm_group(16 * 128, 32)
                P("act", lambda e: e.activation(out=rT[tb][0:32, :], in_=a[0:32, :], func=AF.Copy), reads=[ab], writes=[rT_b[tb]])
                yield
                for j in range(4):
                    tok = slice(j * 128, (j + 1) * 128)
                    a, ab = next_acc()
                    for k in range(8):
                        P("pe", lambda e: e.matmul(a[:], lhsT=hnT[tb][:, k, tok], rhs=win[:, k, 2080:2592], start=(k == 0), stop=(k == 7)),
                          reads=[win_b, hnT_b[tb]], writes=[ab])
                    vb = vcnt[0] % 2
                    vcnt[0] += 1
                    P("act", lambda e: e.activation(out=vst[vb][:, :, 0:64], in_=a[:].rearrange("p (h d) -> p h d", h=8), func=AF.Copy),
                      reads=[ab], writes=[vst_b[vb]])
                    r0 = PADR + t0 + j * 128
                    S.dma("sp", [(sc["v"][r0:r0 + 128, :], vst[vb][:].rearrange("p h d -> p (h d)"))], vst_b[vb], reads=[vst_b[vb]])
                    tick()
                    yield
                    a, ab = next_acc()
                    for k in range(8):
                        P("pe", lambda e: e.matmul(a[:], lhsT=hnT[tb][:, k, tok], rhs=win[:, k, 2592:3104], start=(k == 0), stop=(k == 7)),
                          reads=[win_b, hnT_b[tb]], writes=[ab])
                    P("act", lambda e: e.activation(out=vga[tb][:, j, :], in_=a[:], func=AF.Copy), reads=[ab], writes=[vga_b[tb][j]])
                    tick()
                    yield
                    a, ab = next_acc()
                    for k in range(8):
                        P("pe", lambda e: e.matmul(a[:, 0:256], lhsT=hnT[tb][:, k, tok], rhs=win[:, k, 3104:3360], start=(k == 0), stop=(k == 7)),
                          reads=[win_b, hnT_b[tb]], writes=[ab])
                    P("dve", lambda e: e.tensor_copy(out=kgt[tb][:, j, :], in_=a[:, 0:256]), reads=[ab], writes=[kgt_b[tb][j]])
                    tick()
                    yield

            xslot_a = {}

            def load_xt(si, t, j):
                b = xcount[0] % 3
                xcount[0] += 1
                xslot_a[(si, t, j)] = b
                r0 = t * 512 + j * 128
                S.dma("sp", [(xt[b][:], xs[si][r0:r0 + 128, :])], xt_b[b], writes=[xt_b[b]])

            def norm_ops(si, t, j):
                xb = xslot_a[(si, t, j)]
                sj = j % 2
                P("act", lambda e: e.activation(out=hn[sj][:], in_=xt[xb][:], func=AF.Square, accum_out=ss[:, sj:sj + 1]),
                  reads=[xt_b[xb]], writes=[hn_b[sj], ss_b[sj]])
                P("act", lambda e: e.activation(out=rstd[:, sj:sj + 1], in_=ss[:, sj:sj + 1], func=AF.Ln, scale=1.0 / D, bias=epsb[:]),
                  reads=[ss_b[sj], eps_b], writes=[rstd_b[sj]])
                P("act", lambda e: e.activation(out=rstd[:, sj:sj + 1], in_=rstd[:, sj:sj + 1], func=AF.Exp, scale=-0.5),
                  reads=[rstd_b[sj]], writes=[rstd_b[sj]])
                P("dve", lambda e: e.scalar_tensor_tensor(out=hn[sj][:], in0=xt[xb][:], scalar=rstd[:, sj:sj + 1], in1=g1t[:], op0=ALU.mult, op1=ALU.mult),
                  reads=[xt_b[xb], rstd_b[sj], g1_b], writes=[hn_b[sj]])

            def transp_ops(t, j):
                tb = t % 2
                sj = j % 2
                for k in range(8):
                    P("pe", lambda e: e.transpose(out=pT16[:, k * 128:(k + 1) * 128], in_=hn[sj][:, k * 128:(k + 1) * 128], identity=ident[:]),
                      reads=[hn_b[sj], ident_b], writes=[pT_b])
                P("dve", lambda e: e.tensor_copy(out=hnT[tb][:, :, j * 128:(j + 1) * 128], in_=pT16.rearrange("p (k t) -> p k t", k=8)), reads=[pT_b], writes=[hnT_b[tb]])

            def Y(si, t):
                sc = scr[si]
                tb = t % 2
                t0 = t * 512

                def Ya(j):
                    ch = t * 4 + j
                    tok = slice(j * 128, (j + 1) * 128)
                    kb = j % 2
                    P("pe", lambda e: e.matmul(g1p[:], lhsT=rT[tb][0:33, tok], rhs=upbm[0:33, :], start=True, stop=True), reads=[rT_b[tb], upbm_b], writes=[g1p_b])
                    P("act", lambda e: e.activation(out=Lt[:], in_=g1p[:], func=AF.Exp, scale=-1.0), reads=[g1p_b], writes=[Lt_b])
                    P("act", lambda e: e.activation(out=Lt[:], in_=Lt[:], func=AF.Ln, bias=oneb[:]), reads=[Lt_b, one_b], writes=[Lt_b])
                    yield
                    P("pe", lambda e: e.matmul(g1p[:, 0:256], lhsT=cf[:, 4, :], rhs=Lt[:, 0:256], start=True, stop=True), reads=[cf_b, Lt_b], writes=[g1p_b])
                    P("pe", lambda e: e.matmul(g1p[:, 256:512], lhsT=cf[:, 2, :], rhs=Lt[:, 256:512], start=True, stop=True), reads=[cf_b, Lt_b], writes=[g1p_b])
                    P("act", lambda e: e.activation(out=ek[:].rearrange("p a b -> p (a b)"), in_=g1p[:], func=AF.Exp, scale=-1.0 / 16), reads=[g1p_b], writes=[ek_b])
                    P("dve", lambda e: e.tensor_tensor(out=kend[kb][:], in0=ek[:], in1=kgt[tb][:, j, :].unsqueeze(1).to_broadcast([128, 2, 256]), op=ALU.mult),
                      reads=[ek_b, kgt_b[tb][j]], writes=[kend_b[kb]])
                    yield
                    for d_ in range(2):
                        for g in range(2):
                            col = d_ * 256 + g * 128
                            P("pe", lambda e: e.matmul(g2p[:, (d_ * 2 + g) * 128:(d_ * 2 + g + 1) * 128], lhsT=Lt[:, col:col + 128], rhs=cf[:, 1 if d_ == 0 else 3, :], start=True, stop=True),
                              reads=[cf_b, Lt_b], writes=[g2p_b])
                    P("act", lambda e: e.activation(out=epm[:, 0, :, :].rearrange("p a t -> p (a t)"), in_=g2p[:], func=AF.Exp, scale=-1.0 / 16), reads=[g2p_b], writes=[epm_b])
                    P("act", lambda e: e.activation(out=epm[:, 1, :, :].rearrange("p a t -> p (a t)"), in_=g2p[:], func=AF.Exp, scale=1.0 / 16), reads=[g2p_b], writes=[epm_b])
                    yield
                    P("dve", lambda e: e.tensor_tensor(out=qe[tb][:, :, :, tok], in0=epm[:, 0, :, :].rearrange("p (d g) t -> p d g t", d=2),
                                                       in1=qgs[tb][:, :, tok].unsqueeze(1).to_broadcast([128, 2, 2, 128]), op=ALU.mult),
                      reads=[epm_b, qgs_b[tb]], writes=[qe_b[tb]])
                    for hh_ in range(2):
                        pr_ = slice(64 * hh_, 64 * hh_ + 64)
                        P("dve", lambda e: e.tensor_tensor(out=ke[kb][pr_, hh_, :, :, :], in0=epm[pr_, 1, :, :].rearrange("p (d g) t -> p d g t", d=2),
                                                           in1=kgs[tb][pr_, :, tok].unsqueeze(1).to_broadcast([64, 2, 2, 128]), op=ALU.mult),
                          reads=[epm_b, kgs_b[tb]], writes=[ke_b[kb]])
                    P("act", lambda e: e.activation(out=decf[kb][:], in_=epm[:, 0, 0:2, 127], func=AF.Copy), reads=[epm_b], writes=[decf_b[kb]])
                    P("act", lambda e: e.activation(out=decb[si][:, ch, :], in_=epm[:, 0, 2:4, 0], func=AF.Copy), reads=[epm_b], writes=[decb_b[si]])
                    yield

                def Yb(j):
                    tok = slice(j * 128, (j + 1) * 128)
                    kb = j % 2
                    vgs = vga[tb][:, j, :]
                    vgs_b = vga_b[tb][j]
                    for g in range(2):
                        for hh in range(2):
                            for d_ in range(2):
                                P("pe", lambda e: e.matmul(g3p[:, (hh * 2 + d_) * 128:(hh * 2 + d_ + 1) * 128], lhsT=ke[kb][:, hh, d_, g, :], rhs=qe[tb][:, d_, g, tok], start=True, stop=True),
                                  reads=[ke_b[kb], qe_b[tb]], writes=[g3p_b])
                        P("dve", lambda e: e.tensor_tensor(out=ats[:], in0=g3p[:].rearrange("p (h d c) -> p h d c", h=2, d=2), in1=gmask[:], op=ALU.mult),
                          reads=[g3p_b, gmask_b], writes=[ats_b])
                        if g == 0:
                            for g_ in range(2):
                                P("pe", lambda e: e.matmul(g5p[:, g_ * 256:(g_ + 1) * 256], lhsT=kend[kb][:, 0, g_ * 128:(g_ + 1) * 128], rhs=vgs[:, g_ * 256:(g_ + 1) * 256], start=True, stop=True),
                                  reads=[kend_b[kb], vgs_b], writes=[g5p_b])
                        yield
                        for hh in range(2):
                            h = g * 2 + hh
                            oo = g4p[:, h * 128:(h + 1) * 128]
                            P("pe", lambda e: e.matmul(oo, lhsT=vgs[:, h * 128:(h + 1) * 128], rhs=ats[:, hh, 0, :], start=True, stop=False), reads=[vgs_b, ats_b], writes=[g4p_b])
                            P("pe", lambda e: e.matmul(oo, lhsT=vgs[:, h * 128:(h + 1) * 128], rhs=ats[:, hh, 1, :], start=False, stop=False), reads=[vgs_b, ats_b], writes=[g4p_b])
                            P("pe", lambda e: e.matmul(oo, lhsT=stfb[:, hh, g, :], rhs=qe[tb][:, 0, g, tok], start=False, stop=True), reads=[stfb_b, qe_b[tb]], writes=[g4p_b])
                        if g == 0:
                            yield
                    P("act", lambda e: e.activation(out=oPs[:, :, tok], in_=g4p[:].rearrange("p (h c) -> p h c", h=4), func=AF.Copy), reads=[g4p_b], writes=[oPs_b])
                    for g in range(2):
                        for hh in range(2):
                            pr = slice(64 * hh, 64 * hh + 64)
                            src = g5p[pr, g * 256 + hh * 128:g * 256 + hh * 128 + 128]
                            P("dve", lambda e: e.scalar_tensor_tensor(out=stf[pr, g, :], in0=stf[pr, g, :], scalar=decf[kb][pr, g:g + 1], in1=src, op0=ALU.mult, op1=ALU.add),
                              reads=[stf_b, decf_b[kb], g5p_b], writes=[stf_b])
                    for hh_ in range(2):
                        pr_ = slice(64 * hh_, 64 * hh_ + 64)
                        P("act", lambda e: e.activation(out=stfb[pr_, hh_, :, :], in_=stf[pr_, :, :], func=AF.Copy), reads=[stf_b], writes=[stfb_b])
                    yield
                    for g in range(2):
                        P("pe", lambda e: e.matmul(g5p[:, g * 256:(g + 1) * 256], lhsT=kend[kb][:, 1, g * 128:(g + 1) * 128], rhs=vgs[:, g * 256:(g + 1) * 256], start=True, stop=True),
                          reads=[kend_b[kb], vgs_b], writes=[g5p_b])
                    for g in range(2):
                        for hh in range(2):
                            pr = slice(64 * hh, 64 * hh + 64)
                            src = g5p[pr, g * 256 + hh * 128:g * 256 + hh * 128 + 128]
                            P("act", lambda e: e.activation(out=kvbs[pr, j, g, :], in_=src, func=AF.Copy), reads=[g5p_b], writes=[kvbs_b])
                    yield

                def zipgen(ga, gb):
                    gs = [g for g in (ga, gb) if g is not None]
                    while gs:
                        for g in list(gs):
                            try:
                                next(g)
                            except StopIteration:
                                gs.remove(g)
                                continue
                            yield

                yield from zipgen(Ya(0), None)
                for j in range(4):
                    yield from zipgen(Ya(j + 1) if j + 1 < 4 else None, Yb(j))
                S.dma("sp", [(sc["oP"][h, :, t0:t0 + 512], oPs[:, h, :]) for h in range(4)], oPs_b, reads=[oPs_b])
                S.dma("sp", [(sc["qeb"][g, :, t0:t0 + 512], qe[tb][:, 1, g, :]) for g in range(2)], qe_b[tb], reads=[qe_b[tb]])
                S.dma("sp", [(sc["kvb"][t, :, :], kvbs[:].rearrange("p a g d -> p (a g d)"))], kvbs_b, reads=[kvbs_b])
                yield

            def run_interleaved(gens):
                gens = [g for g in gens if g is not None]
                while gens:
                    for g in list(gens):
                        try:
                            next(g)
                        except StopIteration:
                            gens.remove(g)

            for si, slen in enumerate(seq_lens):
                ntile = slen // 512
                P("pool", lambda e: e.memset(stf[:], 0.0), writes=[stf_b])
                P("pool", lambda e: e.memset(stfb[:], 0.0), writes=[stfb_b])
                for j in range(3):
                    load_xt(si, 0, j)
                for j in range(4):
                    norm_ops(si, 0, j)
                    transp_ops(0, j)
                    if j == 0:
                        load_xt(si, 0, 3)
                run_interleaved([X(si, 0)])
                for t in range(ntile):
                    run_interleaved([X(si, t + 1) if t + 1 < ntile else None, Y(si, t)])
            S.barrier()

        if "B" in phases:
          with contextlib.ExitStack() as pb:
            SMAX = max(seq_lens)
            KW = SMAX + 2 * PADR
            qT2 = [sbt(pb, "qT2_%d" % i, [128, SMAX], BF16) for i in range(2)]; qT2_b = [Buf("qT2_0"), Buf("qT2_1")]
            qd4 = sbt(pb, "qd4", [128, SMAX], BF16); qd4_b = Buf("qd4")
            qd16 = sbt(pb, "qd16", [128, SMAX], BF16); qd16_b = Buf("qd16")
            kAB = [sbt(pb, "k%s" % n, [128, KW], BF16) for n in "AB"]
            kAB_b = [Buf("k%s" % n) for n in "AB"]
            accs = [sbt(pb, "acc%d" % i, [128, SMAX], F32) for i in range(2)]
            accs_cb = [[Buf("acc%d_%d" % (i, c_)) for c_ in range(SMAX // 512)] for i in range(2)]
            NV = 6
            vts = [sbt(pb, "vt%d" % i, [128, 2, 128], BF16) for i in range(NV)]; vts_b = [Buf("vt%d" % i) for i in range(NV)]
            NS = 4
            mixo = [sbt(pb, "mixo%d" % i, [128, SMAX // 4], BF16) for i in range(4)]; mixo_b = [Buf("mixo%d" % i) for i in range(4)]
            deferred = []
            ptm = [sbt(pb, "ptm%d" % i, [128, 2, 256], BF16) for i in range(NS)]; ptm_b = [Buf("ptm%d" % i) for i in range(NS)]
            am2 = sbt(pb, "am2", [128, 256], BF16); am2_b = Buf("am2")
            rdp = pst(pb, "rdp", [128, 512], F32); rdp_b = Buf("rdp")
            lnt, lnt_b = rdp, rdp_b
            rds = [sbt(pb, "rds%d" % i, [64, 512], F32) for i in range(2)]; rds_b = [Buf("rds0"), Buf("rds1")]
            nrc = [0]
            sTp = [pst(pb, "sT%d" % i, [128, 2, 256], F32) for i in range(NS)]; sTp_b = [Buf("sT%d" % i) for i in range(NS)]
            NSL = 3
            oPp = [pst(pb, "oPp%d" % i, [128, 2, 256], F32) for i in range(NSL)]
            oPp_b = [Buf("oPp%d" % i) for i in range(NSL)]

            P("dve", lambda e: e.tensor_scalar(out=am2[:], in0=amask[:], scalar1=-1.0, scalar2=240000.0, op0=ALU.add, op1=ALU.mult), reads=[amask_b], writes=[am2_b])
            for n in range(2):
                P("pool", lambda e: e.memset(kAB[n][:], 0.0), writes=[kAB_b[n]])

            pair_list = [(si, p) for si in range(nseq) for p in range(4)]

            def load_q(idx):
                si, p = pair_list[idx]
                slen = seq_lens[si]
                b = idx % 2
                S.dma("sp", [(qT2[b][:, 0:slen], scr[si]["qT"][2 * p:2 * p + 2].rearrange("h d s -> (h d) s"))], qT2_b[b], writes=[qT2_b[b]])

            def load_k(idx):
                si, p = pair_list[idx]
                slen = seq_lens[si]
                sc = scr[si]
                if idx > 0 and pair_list[idx - 1][0] != si:
                    for n in range(2):
                        P("dve", lambda e: e.memset(kAB[n][:, PADR + slen:PADR + slen + PADR], 0.0), writes=[kAB_b[n]])
                S.dma("sp", [(kAB[0][0:64, PADR:PADR + slen], sc["kT"][2 * p])], kAB_b[0], writes=[kAB_b[0]])
                S.dma("sp", [(kAB[1][64:128, PADR:PADR + slen], sc["kT"][2 * p + 1])], kAB_b[1], writes=[kAB_b[1]])

            load_q(0)
            vcount = [0]
            for idx, (si, p) in enumerate(pair_list):
                slen = seq_lens[si]
                sc = scr[si]
                b = idx % 2
                S4 = slen // 4
                if idx == 0:
                    load_k(0)
                def destride4(idx_):
                    si_, _p = pair_list[idx_]
                    s4_ = seq_lens[si_] // 4
                    for r in range(4):
                        P("pool", lambda e: e.tensor_copy(out=qd4[:, r * s4_:(r + 1) * s4_], in_=qT2[idx_ % 2][:, sl(r, s4_, 4)]), reads=[qT2_b[idx_ % 2]], writes=[qd4_b])

                if idx == 0:
                    destride4(0)
                S16 = slen // 16
                for r in range(16):
                    P("pool", lambda e: e.tensor_copy(out=qd16[:, r * S16:(r + 1) * S16], in_=qT2[b][:, sl(r, S16, 16)]), reads=[qT2_b[b]], writes=[qd16_b])
                if idx + 1 < len(pair_list):
                    load_q(idx + 1)
                items = []
                for d in (1, 4, 16):
                    L = slen // d
                    NQ = L // 128
                    for r in range(d):
                        for j in range(NQ + 1):
                            items.append((d, r, j, NQ))
                vslot = {}

                def load_v(ii):
                    d, r, j, NQ = items[ii]
                    vb = vcount[0] % NV
                    vcount[0] += 1
                    vslot[ii] = vb
                    row0 = PADR + r + d * (128 * j - 64)
                    src = sc["v"][sl(row0, 128, d), 2 * p * 128:(2 * p + 2) * 128]
                    S.dma("sp", [(vts[vb][:].rearrange("p h c -> p (h c)"), src)], vts_b[vb], writes=[vts_b[vb]])

                def qrange(d, r, j, NQ):
                    lo = max(j - 1, 0)
                    hi = min(j, NQ - 1)
                    c_lo = (lo - (j - 1)) * 128
                    c_hi = (hi - (j - 1) + 1) * 128
                    return lo, hi, c_lo, c_hi

                def scores(ii):
                    d, r, j, NQ = items[ii]
                    lo, hi, c_lo, c_hi = qrange(d, r, j, NQ)
                    sb_ = ii % NS
                    k0 = PADR + r + d * (128 * j - 64)
                    nq = (hi - lo + 1) * 128
                    if d == 1:
                        qsrc, qb_ = qT2[b][:, 128 * lo:128 * lo + nq], qT2_b[b]
                    elif d == 4:
                        qsrc, qb_ = qd4[:, r * S4 + 128 * lo:r * S4 + 128 * lo + nq], qd4_b
                    else:
                        qsrc, qb_ = qd16[:, r * S16 + 128 * lo:r * S16 + 128 * lo + nq], qd16_b
                    for hh in range(2):
                        P("pe", lambda e: e.matmul(sTp[sb_][:, hh, c_lo:c_hi], lhsT=kAB[hh][:, sl(k0, 128, d)], rhs=qsrc, start=True, stop=False),
                          reads=[kAB_b[hh], qb_], writes=[sTp_b[sb_]])
                        P("pe", lambda e: e.matmul(sTp[sb_][:, hh, c_lo:c_hi], lhsT=ident[:], rhs=am2[:, c_lo:c_hi], start=False, stop=True),
                          reads=[ident_b, am2_b], writes=[sTp_b[sb_]])

                def acc_bufs(hh, d, r, qt):
                    if d == 1:
                        return [accs_cb[hh][(r4 * S4 + 32 * qt) // 512] for r4 in range(4)]
                    if d == 4:
                        return [accs_cb[hh][(r * S4 + 128 * qt) // 512]]
                    r4 = r % 4
                    return [accs_cb[hh][(r4 * S4 + 4 * 128 * qt) // 512]]

                def acc_view(hh, d, r, qt):
                    a3 = accs[hh][:, 0:slen].rearrange("p (r m) -> p r m", r=4)
                    if d == 1:
                        return a3[:, :, 32 * qt:32 * qt + 32]
                    if d == 4:
                        return accs[hh][:, r * S4 + 128 * qt:r * S4 + 128 * qt + 128]
                    r4, bq = r % 4, r // 4
                    return accs[hh][:, sl(r4 * S4 + 4 * 128 * qt + bq, 128, 4)]

                def st_exp(ii):
                    d, r, j, NQ = items[ii]
                    lo, hi, c_lo, c_hi = qrange(d, r, j, NQ)
                    sb_ = ii % NS
                    P("act", lambda e: e.activation(out=ptm[sb_][:, :, c_lo:c_hi], in_=sTp[sb_][:, :, c_lo:c_hi], func=AF.Exp, scale=0.125),
                      reads=[sTp_b[sb_]], writes=[ptm_b[sb_]])

                def st_mask(ii):
                    return

                    d, r, j, NQ = items[ii]
                    lo, hi, c_lo, c_hi = qrange(d, r, j, NQ)
                    sb_ = ii % NS
                    P("dve", lambda e: e.tensor_tensor(out=ptm[sb_][:, :, c_lo:c_hi], in0=pts[sb_][:, :, c_lo:c_hi], in1=am2[:, :, c_lo:c_hi], op=ALU.mult),
                      reads=[pts_b[sb_], am2_b], writes=[ptm_b[sb_]])

                def st_pv(ii):
                    d, r, j, NQ = items[ii]
                    lo, hi, c_lo, c_hi = qrange(d, r, j, NQ)
                    sb_ = ii % NS
                    vb = vslot[ii]
                    for qt in range(lo, hi + 1):
                        cq = (qt - (j - 1)) * 128
                        for hh in range(2):
                            P("pe", lambda e: e.matmul(oPp[qt % NSL][:, hh, 0:128], lhsT=vts[vb][:, hh, :], rhs=ptm[sb_][:, hh, cq:cq + 128],
                                                       start=(qt == j and hh == 0), stop=(qt == j - 1 and hh == 1)),
                              reads=[vts_b[vb], ptm_b[sb_]], writes=[oPp_b[qt % NSL]])
                    if j >= 1:
                        qt = j - 1
                        for hh in range(2):
                            src = oPp[qt % NSL][:, hh, 0:128]
                            if d == 1:
                                P("dve", lambda e: e.tensor_copy(out=acc_view(hh, d, r, qt), in_=src.rearrange("p (m r) -> p r m", r=4)),
                                  reads=[oPp_b[qt % NSL]], writes=acc_bufs(hh, d, r, qt))
                            else:
                                P("dve", lambda e: e.tensor_tensor(out=acc_view(hh, d, r, qt), in0=acc_view(hh, d, r, qt), in1=src, op=ALU.add),
                                  reads=[oPp_b[qt % NSL]] + acc_bufs(hh, d, r, qt), writes=acc_bufs(hh, d, r, qt))

                NI = len(items)
                for ii in range(min(4, NI)):
                    load_v(ii)
                for fn_ in deferred:
                    fn_()
                del deferred[:]
                for step in range(-3, NI):
                    if step >= 0 and step + 4 < NI:
                        load_v(step + 4)
                    if 0 <= step + 3 < NI:
                        scores(step + 3)
                    if 0 <= step + 2 < NI:
                        st_exp(step + 2)
                    if 0 <= step + 1 < NI:
                        st_mask(step + 1)
                    if 0 <= step < NI:
                        st_pv(step)
                    if idx + 1 < len(pair_list) and 0 <= step + 3 < NI and items[step + 3][0] == 16 and (step + 3 == 0 or items[step + 2][0] != 16):
                        destride4(idx + 1)
                if idx + 1 < len(pair_list):
                    load_k(idx + 1)
                mixn, mixn_b = qd16, qd16_b
                nchk = slen // 512
                per_r = nchk // 4
                order = [r4 * per_r + i for i in range(per_r) for r4 in range(4)]
                for blk in order:
                    cs_ = slice(blk * 512, (blk + 1) * 512)
                    for hh in range(2):
                        nb_ = nrc[0] % 2
                        nrc[0] += 1
                        P("act", lambda e: e.activation(out=lnt[64:128, :], in_=accs[hh][64:128, cs_], func=AF.Ln), reads=[accs_cb[hh][blk]], writes=[lnt_b])
                        P("act", lambda e: e.activation(out=rds[nb_][0:64, :], in_=lnt[64:128, :], func=AF.Exp, scale=-1.0), reads=[lnt_b], writes=[rds_b[nb_]])
                        P("dve", lambda e: e.tensor_tensor(out=mixn[64 * hh:64 * hh + 64, cs_], in0=accs[hh][0:64, cs_], in1=rds[nb_][0:64, :], op=ALU.mult),
                          reads=[accs_cb[hh][blk], rds_b[nb_]], writes=[mixn_b])
                NCH_ = 4
                mch = S4 // NCH_
                for c_ in range(NCH_):
                    mo = c_
                    for r in range(4):
                        P("pool", lambda e: e.tensor_copy(out=mixo[mo][:, sl(r, mch, 4)], in_=mixn[:, r * S4 + c_ * mch:r * S4 + (c_ + 1) * mch]),
                          reads=[mixn_b], writes=[mixo_b[mo]])
                    deferred.append((lambda sc=sc, p=p, c_=c_, mch=mch, mo=mo: S.dma("sp", [(sc["mixA"][p, :, 4 * c_ * mch:4 * (c_ + 1) * mch], mixo[mo][:, 0:4 * mch])], mixo_b[mo], reads=[mixo_b[mo]])))
            for fn_ in deferred:
                fn_()
            del deferred[:]
            S.barrier()

        wg = sbt(G, "wg", [128, 8, DFF], BF16); wg_b = Buf("wg")
        if "D" in phases:
            S.dma("pool", [(wg[:, k, :], wg_d[k * 128:(k + 1) * 128, :]) for k in range(8)], wg_b, writes=[wg_b])

        if "C" in phases:
          with contextlib.ExitStack() as pc:
            wout = sbt(pc, "wout", [128, 8, D], BF16); wout_b = Buf("wout")
            S.dma("pool", [(wout[:, k, :], wout_d[k * 128:(k + 1) * 128, :]) for k in range(8)], wout_b, writes=[wout_b])
            oPt = [sbt(pc, "oPt%d" % i, [128, 4, 512], F32) for i in range(2)]; oPt_b = [Buf("oPt0"), Buf("oPt1")]
            qebt = [sbt(pc, "qebt%d" % i, [128, 2, 512], BF16) for i in range(2)]; qebt_b = [Buf("qebt0"), Buf("qebt1")]
            kvbt = [sbt(pc, "kvbt%d" % i, [128, 4, 2, 128], F32) for i in range(2)]; kvbt_b = [Buf("kvbt0"), Buf("kvbt1")]
            sogt = [sbt(pc, "sogt%d" % i, [128, 4, 512], BF16) for i in range(2)]; sogt_b = [Buf("sogt0"), Buf("sogt1")]
            mixAt = [sbt(pc, "mixAt%d" % i, [128, 4, 512], BF16) for i in range(2)]; mixAt_b = [Buf("mixAt0"), Buf("mixAt1")]
            xc = [sbt(pc, "xc%d" % i, [128, D], F32) for i in range(8)]; xc_b = [Buf("xc%d" % i) for i in range(8)]
            mixG = [sbt(pc, "mixG%d" % i, [128, 4, 512], BF16) for i in range(2)]; mixG_b = [Buf("mixG0"), Buf("mixG1")]
            stb = sbt(pc, "stb", [128, 2, 128], F32); stb_b = Buf("stb")
            stbb = [sbt(pc, "stbb%d" % i, [128, 2, 2, 128], BF16) for i in range(4)]; stbb_b = [Buf("stbb%d" % i) for i in range(4)]
            osum = [sbt(pc, "osum%d" % i, [128, 4, 128], F32) for i in range(2)]; osum_b = [Buf("osum0"), Buf("osum1")]
            osq = sbt(pc, "osq", [128, 4, 128], BF16); osq_b = Buf("osq")
            grs = sbt(pc, "grs", [128, 4, 128], F32); grs_b = Buf("grs")
            OTp = [pst(pc, "OTp%d" % i, [128, 4, 128], F32) for i in range(2)]; OTp_b = [Buf("OTp0"), Buf("OTp1")]
            SSp = pst(pc, "SSp", [128, 4, 128], F32); SSp_b = Buf("SSp")
            Yp = [pst(pc, "Yp%d" % i, [128, 512], F32) for i in range(4)]; Yp_b = [Buf("Yp%d" % i) for i in range(4)]
            ycount = [0]

            def load_tile(si, t):
                sc = scr[si]
                tb = t % 2
                t0 = t * 512
                S.dma("sp", [(oPt[tb][:, h, :], sc["oP"][h, :, t0:t0 + 512]) for h in range(4)], oPt_b[tb], writes=[oPt_b[tb]])
                S.dma("sp", [(qebt[tb][:, g, :], sc["qeb"][g, :, t0:t0 + 512]) for g in range(2)], qebt_b[tb], writes=[qebt_b[tb]])
                S.dma("sp", [(kvbt[tb][:].rearrange("p a g d -> p (a g d)"), sc["kvb"][t, :, :])], kvbt_b[tb], writes=[kvbt_b[tb]])
                S.dma("sp", [(sogt[tb][:, h, :], sc["sog"][h, :, t0:t0 + 512]) for h in range(4)], sogt_b[tb], writes=[sogt_b[tb]])
                S.dma("sp", [(mixAt[tb][:, h, :], sc["mixA"][h, :, t0:t0 + 512]) for h in range(4)], mixAt_b[tb], writes=[mixAt_b[tb]])
                for j in range(4):
                    xb = (t % 2) * 4 + j
                    S.dma("sp", [(xc[xb][:], xs[si][t0 + j * 128:t0 + (j + 1) * 128, :])], xc_b[xb], writes=[xc_b[xb]])

            def Yc(si, t):
                tb = t % 2

                def states(j):
                    ch = t * 4 + j
                    for hh in range(2):
                        pr = slice(64 * hh, 64 * hh + 64)
                        P("dve", lambda e: e.tensor_copy(out=stbb[j][pr, hh, :, :], in_=stb[pr, :, :]), reads=[stb_b], writes=[stbb_b[j]])
                    for g in range(2):
                        P("dve", lambda e: e.scalar_tensor_tensor(out=stb[:, g, :], in0=stb[:, g, :], scalar=decb[si][:, ch, g:g + 1], in1=kvbt[tb][:, j, g, :], op0=ALU.mult, op1=ALU.add),
                          reads=[stb_b, decb_b[si], kvbt_b[tb]], writes=[stb_b])

                def s1(j):
                    tok = slice(j * 128, (j + 1) * 128)
                    ob = j % 2
                    for g in range(2):
                        for hh in range(2):
                            h = g * 2 + hh
                            P("pe", lambda e: e.matmul(OTp[ob][:, h, :], lhsT=stbb[j][:, hh, g, :], rhs=qebt[tb][:, g, tok], start=True, stop=True),
                              reads=[stbb_b[j], qebt_b[tb]], writes=[OTp_b[ob]])
                    P("dve", lambda e: e.tensor_tensor(out=osum[ob][:], in0=oPt[tb][:, :, tok], in1=OTp[ob][:], op=ALU.add), reads=[oPt_b[tb], OTp_b[ob]], writes=[osum_b[ob]])

                def s2(j):
                    tok = slice(j * 128, (j + 1) * 128)
                    ob = j % 2
                    P("act", lambda e: e.activation(out=osq[:], in_=osum[ob][:], func=AF.Square), reads=[osum_b[ob]], writes=[osq_b])
                    P("pe", lambda e: e.matmul(SSp[:].rearrange("p h c -> p (h c)"), lhsT=ones_bf[:], rhs=osq[:].rearrange("p h c -> p (h c)"), start=True, stop=True),
                      reads=[ones_b, osq_b], writes=[SSp_b])
                    P("act", lambda e: e.activation(out=grs[:], in_=SSp[:], func=AF.Ln, scale=1.0 / 128, bias=epsb[:]), reads=[SSp_b, eps_b], writes=[grs_b])
                    P("act", lambda e: e.activation(out=grs[:], in_=grs[:], func=AF.Exp, scale=-0.5), reads=[grs_b], writes=[grs_b])
                    P("dve", lambda e: e.tensor_tensor(out=osum[ob][:], in0=osum[ob][:], in1=grs[:], op=ALU.mult), reads=[osum_b[ob], grs_b], writes=[osum_b[ob]])
                    for h in range(4):
                        P("dve", lambda e: e.scalar_tensor_tensor(out=mixG[tb][:, h, tok], in0=osum[ob][:, h, :], scalar=ggs[:, h:h + 1], in1=sogt[tb][:, h, tok], op0=ALU.mult, op1=ALU.mult),
                          reads=[osum_b[ob], ggs_b, sogt_b[tb]], writes=[mixG_b[tb]])

                states(3); yield
                states(2); s1(3); yield
                states(1); s1(2); yield
                s2(3); yield
                states(0); s1(1); yield
                s2(2); yield
                s1(0); yield
                s2(1); yield
                s2(0); yield

            def Xc(si, t):
                sc = scr[si]
                tb = t % 2
                t0 = t * 512
                for j in range(4):
                    tok = slice(j * 128, (j + 1) * 128)
                    xb = (t % 2) * 4 + j
                    for c in range(2):
                        yi = ycount[0] % 4
                        ycount[0] += 1
                        for kc in range(8):
                            lhs = mixAt[tb][:, kc, tok] if kc < 4 else mixG[tb][:, kc - 4, tok]
                            P("pe", lambda e: e.matmul(Yp[yi][:], lhsT=lhs, rhs=wout[:, kc, c * 512:(c + 1) * 512], start=(kc == 0), stop=(kc == 7)),
                              reads=[mixAt_b[tb], mixG_b[tb], wout_b], writes=[Yp_b[yi]])
                        P("dve", lambda e: e.tensor_tensor(out=xc[xb][:, c * 512:(c + 1) * 512], in0=xc[xb][:, c * 512:(c + 1) * 512], in1=Yp[yi][:], op=ALU.add),
                          reads=[xc_b[xb], Yp_b[yi]], writes=[xc_b[xb]])
                        yield
                    S.dma("sp", [(sc["x1"][t0 + j * 128:t0 + (j + 1) * 128, :], xc[xb][:])], xc_b[xb], reads=[xc_b[xb]])

            def run_interleaved_c(gens):
                gens = [g for g in gens if g is not None]
                while gens:
                    for g in list(gens):
                        try:
                            next(g)
                        except StopIteration:
                            gens.remove(g)

            for si, slen in enumerate(seq_lens):
                ntile = slen // 512
                P("pool", lambda e: e.memset(stb[:], 0.0), writes=[stb_b])
                for i_ in range(4):
                    P("pool", lambda e: e.memset(stbb[i_][:], 0.0), writes=[stbb_b[i_]])
                load_tile(si, ntile - 1)
                if ntile >= 2:
                    load_tile(si, ntile - 2)
                run_interleaved_c([Yc(si, ntile - 1)])
                for t in range(ntile - 1, -1, -1):
                    run_interleaved_c([Xc(si, t), Yc(si, t - 1) if t - 1 >= 0 else None])
                    if t - 2 >= 0:
                        load_tile(si, t - 2)
            S.barrier()

        if "D" in phases:
          with contextlib.ExitStack() as pd:
            wu = sbt(pd, "wu", [128, 8, DFF], BF16); wu_b = Buf("wu")
            wd = sbt(pd, "wd", [128, NFF, D], BF16); wd_b = Buf("wd")
            g2t = sbt(pd, "g2t", [128, D], F32); g2_b = Buf("g2t")
            gft = sbt(pd, "gft", [128, D], F32); gf_b = Buf("gft")
            S.dma("pool", [(wu[:, k, :], wu_d[k * 128:(k + 1) * 128, :]) for k in range(8)], wu_b, writes=[wu_b])
            S.dma("pool", [(wd[:, k, :], wd_d[k * 128:(k + 1) * 128, :]) for k in range(NFF)], wd_b, writes=[wd_b])
            S.dma("sp", [(g2t[:], g2_d.partition_broadcast(128))], g2_b, writes=[g2_b])
            S.dma("sp", [(gft[:], gf_d.partition_broadcast(128))], gf_b, writes=[gf_b])
            NX = 6
            xd = [sbt(pd, "xd%d" % i, [128, D], F32) for i in range(NX)]; xd_b = [Buf("xd%d" % i) for i in range(NX)]
            xdst_b = [Buf("xdst%d" % i) for i in range(NX)]
            hn2 = [sbt(pd, "hn2_%d" % i, [128, D], BF16) for i in range(2)]; hn2_b = [Buf("hn2_0"), Buf("hn2_1")]
            hn2T = [sbt(pd, "hn2T%d" % i, [128, 8, 256], BF16) for i in range(2)]; hn2T_b = [Buf("hn2T0"), Buf("hn2T1")]
            hT = sbt(pd, "hT", [128, NFF, 256], BF16); hT_b = Buf("hT")
            sg = [sbt(pd, "sg%d" % i, [128, 256], F32) for i in range(2)]; sg_b = [Buf("sg0"), Buf("sg1")]
            ss2 = sbt(pd, "ss2", [128, 6], F32); ss2_b = [Buf("ss2_%d" % i) for i in range(6)]
            rs2 = sbt(pd, "rs2", [128, 6], F32); rs2_b = [Buf("rs2_%d" % i) for i in range(6)]
            junk2 = sbt(pd, "junk2", [128, D], BF16); junk2_b = Buf("junk2")
            pT2 = pst(pd, "pT2", [128, 1024], BF16); pT2_b = Buf("pT2")
            GU = [pst(pd, "GU%d" % i, [128, 2, 256], F32) for i in range(3)]; GU_b = [Buf("GU%d" % i) for i in range(3)]
            Yd = [pst(pd, "Yd%d" % i, [128, 512], F32) for i in range(4)]; Yd_b = [Buf("Yd%d" % i) for i in range(4)]
            tiles = [(si, t) for si in range(nseq) for t in range(d_lens[si] // 256)]
            idxs = [sbt(pd, "ridxs%d" % i, [128, d_lens[i] // 128], mybir.dt.int32) for i in range(nseq)]; idxs_b = [Buf("ridx%d" % i) for i in range(nseq)]
            for i in range(nseq):
                S.dma("sp", [(idxs[i][:], ridx_d[i])], idxs_b[i], writes=[idxs_b[i]])
            xcount = [0]
            xslot = {}

            def load_x1(ti):
                si, t = tiles[ti]
                for j in range(2):
                    xb = xcount[0] % NX
                    xcount[0] += 1
                    xslot[(ti, j)] = xb
                    k_ = t * 2 + j
                    S.dma_custom("pool", (lambda e, xb=xb, si=si, k_=k_: e.indirect_dma_start(
                        out=xd[xb][:, :], out_offset=None, in_=scr[si]["x1"][:, :],
                        in_offset=bass.IndirectOffsetOnAxis(ap=idxs[si][:, k_:k_ + 1], axis=0))), xd_b[xb], reads=[idxs_b[si]], writes=[xd_b[xb]])

            load_x1(0)
            gcount = [0]
            ycount = [0]

            def norm2_part(ti, j):
                xb = xslot[(ti, j)]
                sj = (ti * 2 + j) % 4
                P("act", lambda e: e.activation(out=junk2[:], in_=xd[xb][:], func=AF.Square, accum_out=ss2[:, sj:sj + 1]), reads=[xd_b[xb]], writes=[junk2_b, ss2_b[sj]])
                P("act", lambda e: e.activation(out=rs2[:, sj:sj + 1], in_=ss2[:, sj:sj + 1], func=AF.Ln, scale=1.0 / D, bias=epsb[:]), reads=[ss2_b[sj], eps_b], writes=[rs2_b[sj]])
                P("act", lambda e: e.activation(out=rs2[:, sj:sj + 1], in_=rs2[:, sj:sj + 1], func=AF.Exp, scale=-0.5), reads=[rs2_b[sj]], writes=[rs2_b[sj]])
                P("dve", lambda e: e.scalar_tensor_tensor(out=hn2[j][:], in0=xd[xb][:], scalar=rs2[:, sj:sj + 1], in1=g2t[:], op0=ALU.mult, op1=ALU.mult),
                  reads=[xd_b[xb], rs2_b[sj], g2_b], writes=[hn2_b[j]])

            def transp_part(ti, j):
                tb = ti % 2
                for k in range(8):
                    P("pe", lambda e: e.transpose(out=pT2[:, k * 128:(k + 1) * 128], in_=hn2[j][:, k * 128:(k + 1) * 128], identity=ident[:]), reads=[hn2_b[j], ident_b], writes=[pT2_b])
                P("dve", lambda e: e.tensor_copy(out=hn2T[tb][:, :, j * 128:(j + 1) * 128], in_=pT2[:].rearrange("p (k t) -> p k t", k=8)), reads=[pT2_b], writes=[hn2T_b[tb]])

            for j in range(2):
                norm2_part(0, j)
                transp_part(0, j)
            for ti, (si, t) in enumerate(tiles):
                if ti + 1 < len(tiles):
                    load_x1(ti + 1)
                tb = ti % 2
                nxt = ti + 1 < len(tiles)
                for f in range(NFF):
                    gi = gcount[0] % 3
                    gcount[0] += 1
                    for k in range(8):
                        P("pe", lambda e: e.matmul(GU[gi][:, 0, :], lhsT=wg[:, k, f * 128:(f + 1) * 128], rhs=hn2T[tb][:, k, :], start=(k == 0), stop=(k == 7)), reads=[wg_b, hn2T_b[tb]], writes=[GU_b[gi]])
                    for k in range(8):
                        P("pe", lambda e: e.matmul(GU[gi][:, 1, :], lhsT=wu[:, k, f * 128:(f + 1) * 128], rhs=hn2T[tb][:, k, :], start=(k == 0), stop=(k == 7)), reads=[wu_b, hn2T_b[tb]], writes=[GU_b[gi]])
                    sb_ = f % 2
                    P("act", lambda e: e.activation(out=sg[sb_][:], in_=GU[gi][:, 0, :], func=AF.Silu), reads=[GU_b[gi]], writes=[sg_b[sb_]])
                    P("dve", lambda e: e.tensor_tensor(out=hT[:, f, :], in0=sg[sb_][:], in1=GU[gi][:, 1, :], op=ALU.mult), reads=[sg_b[sb_], GU_b[gi]], writes=[hT_b])
                    if nxt and f == 6:
                        norm2_part(ti + 1, 0)
                        norm2_part(ti + 1, 1)
                    if nxt and f == 12:
                        transp_part(ti + 1, 0)
                    if nxt and f == 16:
                        transp_part(ti + 1, 1)
                for j in range(2):
                    xb = xslot[(ti, j)]
                    sj = (ti * 2 + j) % 4
                    tok = slice(j * 128, (j + 1) * 128)
                    for c in range(2):
                        yi = ycount[0] % 4
                        ycount[0] += 1
                        for f in range(NFF):
                            P("pe", lambda e: e.matmul(Yd[yi][:], lhsT=hT[:, f, tok], rhs=wd[:, f, c * 512:(c + 1) * 512], start=(f == 0), stop=(f == NFF - 1)), reads=[hT_b, wd_b], writes=[Yd_b[yi]])
                        P("dve", lambda e: e.tensor_tensor(out=xd[xb][:, c * 512:(c + 1) * 512], in0=xd[xb][:, c * 512:(c + 1) * 512], in1=Yd[yi][:], op=ALU.add), reads=[xd_b[xb], Yd_b[yi]], writes=[xd_b[xb]])
                    sjf = 4 + j
                    P("act", lambda e: e.activation(out=junk2[:], in_=xd[xb][:], func=AF.Square, accum_out=ss2[:, sjf:sjf + 1]), reads=[xd_b[xb]], writes=[junk2_b, ss2_b[sjf]])
                    P("act", lambda e: e.activation(out=rs2[:, sjf:sjf + 1], in_=ss2[:, sjf:sjf + 1], func=AF.Ln, scale=1.0 / D, bias=epsb[:]), reads=[ss2_b[sjf], eps_b], writes=[rs2_b[sjf]])
                    P("act", lambda e: e.activation(out=rs2[:, sjf:sjf + 1], in_=rs2[:, sjf:sjf + 1], func=AF.Exp, scale=-0.5), reads=[rs2_b[sjf]], writes=[rs2_b[sjf]])
                    P("dve", lambda e: e.scalar_tensor_tensor(out=xd[xb][:], in0=xd[xb][:], scalar=rs2[:, sjf:sjf + 1], in1=gft[:], op0=ALU.mult, op1=ALU.mult),
                      reads=[xd_b[xb], rs2_b[sjf], gf_b], writes=[xd_b[xb]])
                    r0 = t * 256 + j * 128
                    S.dma("sp", [(ys[si][r0:r0 + 128, :], xd[xb][:])], xdst_b[xb], reads=[xd_b[xb]])
            S.barrier()

        S.emit()
    return nc


def permute_w_in(w_in):
    w = np.asarray(w_in).reshape(D, 3104)
    qa, ka, va = w[:, 0:512], w[:, 512:1024], w[:, 1024:1536]
    qg, kg, vg, og = w[:, 1536:1792], w[:, 1792:2048], w[:, 2048:2560], w[:, 2560:3072]
    rr = w[:, 3072:3104]
    cols = []
    for m in (qa, ka):
        m4 = m.reshape(D, 8, 2, 32)
        for half_heads in (slice(0, 4), slice(4, 8)):
            cols.append(m4[:, half_heads, 0, :].reshape(D, 128))
            cols.append(m4[:, half_heads, 1, :].reshape(D, 128))
    cols += [qg, kg, og, rr]
    cols += [va, vg, kg]
    out = np.concatenate(cols, axis=1)
    return np.ascontiguousarray(out, dtype=np.float32)


_ROPE_THETA = 10000.0


def _rope_tables(s):
    inv = (_ROPE_THETA ** (-np.arange(0, 64, 2, dtype=np.float32) / np.float32(64))).astype(np.float32)
    ang = (np.arange(s, dtype=np.float32)[:, None] * inv[None, :]).astype(np.float32)
    c = np.cos(ang).astype(np.float32).T
    sn = np.sin(ang).astype(np.float32).T
    return np.ascontiguousarray(np.tile(c, (4, 1))), np.ascontiguousarray(np.tile(sn, (4, 1)))


_PROG = {}


def kernel(x_prompt, x_sample, norm1_g, w_in, gate_up_fwd, gate_bias_fwd, gate_up_bwd, gate_bias_bwd,
           gla_norm_g, w_out, norm2_g, w_gate, w_up, w_down, final_norm_g):
    f32 = lambda a: np.ascontiguousarray(np.asarray(a, dtype=np.float32))
    x_prompt = f32(x_prompt)
    x_sample = f32(x_sample)
    nb_p, s_p, _ = x_prompt.shape
    nb_s, s_s, _ = x_sample.shape
    n = 8
    seq_lens = (s_s, s_p)
    split = (n == 2 * nb_p and n == nb_s)
    d_lens = (s_s, s_p // 2) if split else (s_s, s_p)
    if seq_lens not in _PROG:
        _PROG[seq_lens] = build_program(list(seq_lens), d_lens=d_lens)
    nc = _PROG[seq_lens]
    cos0, sin0 = _rope_tables(s_s)
    cos1, sin1 = _rope_tables(s_p)
    shared = {
        "cos0": cos0, "sin0": sin0, "cos1": cos1, "sin1": sin1,
        "w_in_p": permute_w_in(f32(w_in)[0]),
        "w_out": f32(w_out)[0], "w_gate": f32(w_gate)[0], "w_up": f32(w_up)[0], "w_down": f32(w_down)[0],
        "gate_up_fwd": f32(gate_up_fwd)[0], "gate_up_bwd": f32(gate_up_bwd)[0],
        "gate_bias_fwd": f32(gate_bias_fwd).reshape(1, 256), "gate_bias_bwd": f32(gate_bias_bwd).reshape(1, 256),
        "norm1_g": f32(norm1_g).reshape(1, D), "norm2_g": f32(norm2_g).reshape(1, D),
        "final_norm_g": f32(final_norm_g).reshape(1, D),
        "gla_norm_g": np.ascontiguousarray(f32(gla_norm_g).reshape(4, 128).T),
    }
    in_maps = []
    for c in range(n):
        m = dict(shared)
        m["x0"] = x_sample[c % nb_s]
        m["x1"] = x_prompt[c % nb_p]
        half = (c // nb_p) if split else 0
        m["ridx0"] = np.ascontiguousarray(np.arange(d_lens[0], dtype=np.int32).reshape(-1, 128).T)
        m["ridx1"] = np.ascontiguousarray((half * d_lens[1] + np.arange(d_lens[1], dtype=np.int32)).reshape(-1, 128).T)
        in_maps.append(m)
    res = run_bass_kernel_spmd(nc, in_maps, core_ids=list(range(n)))
    y_sample = np.stack([np.asarray(res.results[c]["y0"], dtype=np.float32) for c in range(nb_s)], axis=0)
    if split:
        y_prompt = np.stack([np.concatenate([np.asarray(res.results[c]["y1"], dtype=np.float32),
                                             np.asarray(res.results[c + nb_p]["y1"], dtype=np.float32)], axis=0) for c in range(nb_p)], axis=0)
    else:
        y_prompt = np.stack([np.asarray(res.results[c]["y1"], dtype=np.float32) for c in range(nb_p)], axis=0)
    return (y_prompt, y_sample)
```
